# Optimizing a Trainium2 kernel written in Bass

```python
import math
import jax, jax.numpy as jnp
from jax import lax
import numpy as np

D_MODEL = 1024
BATCH = 16
SEQ = 4096
DEPTH = 2

D_MIX = 2 * D_MODEL
D_CONV = D_MIX // 2
D_LRU = D_MIX - D_CONV
CONV_GROUPS = 8
LRU_HEADS = 8
LRU_HEAD_DIM = D_LRU // LRU_HEADS
CONV_KERNEL = 31
SHORT_CONV = 4
LRU_C = 8.0
D_IN = 3 * D_CONV + 2 * D_LRU
EPS = 1e-6

kernel_name = "hymba_style_conformerconv_rglru_adaln"


def rmsnorm(x, g):
    xf = x.astype(jnp.float32)
    y = xf * lax.rsqrt(jnp.mean(xf * xf, axis=-1, keepdims=True) + EPS)
    return (y * g.astype(jnp.float32)).astype(x.dtype)


def layernorm(x, g, b):
    xf = x.astype(jnp.float32)
    mu = jnp.mean(xf, axis=-1, keepdims=True)
    var = jnp.mean(jnp.square(xf - mu), axis=-1, keepdims=True)
    y = (xf - mu) * lax.rsqrt(var + EPS)
    return (y * g.astype(jnp.float32) + b.astype(jnp.float32)).astype(x.dtype)


def causal_depthwise_conv(v, w, b):
    k, ch = w.shape
    y = lax.conv_general_dilated(
        v, w.astype(v.dtype)[:, None, :], window_strides=(1,),
        padding=[(k - 1, 0)], dimension_numbers=("NWC", "WIO", "NWC"),
        feature_group_count=ch)
    return y + b.astype(v.dtype)


def linear_scan(a, bx):
    def step(h, inp):
        a_t, b_t = inp
        h = a_t * h + b_t
        return h, h
    h0 = jnp.zeros((a.shape[0], a.shape[2]), jnp.float32)
    _, hs = lax.scan(step, h0, (jnp.swapaxes(a, 0, 1), jnp.swapaxes(bx, 0, 1)))
    return jnp.swapaxes(hs, 0, 1)


def setup_inputs(seed: int = 0) -> dict:
    key = jax.random.key(seed)
    ks = jax.random.split(key, 24)
    f32 = jnp.float32
    nrm = lambda k, s, sc: (jax.random.normal(k, s, f32) * sc)
    x = jax.random.normal(ks[0], (BATCH, SEQ, D_MODEL), f32)
    c = jax.random.normal(ks[1], (BATCH, D_MODEL), f32)
    norm_g = 1.0 + nrm(ks[2], (DEPTH, D_MODEL), 0.05)
    mod_w = nrm(ks[3], (DEPTH, D_MODEL, 3 * D_MODEL), 0.5 * D_MODEL ** -0.5)
    mod_b = nrm(ks[4], (DEPTH, 3 * D_MODEL), 0.02)
    w_in = nrm(ks[5], (DEPTH, D_MODEL, D_IN), D_MODEL ** -0.5)
    dw_w = nrm(ks[6], (DEPTH, CONV_KERNEL, D_CONV), CONV_KERNEL ** -0.5)
    dw_b = nrm(ks[7], (DEPTH, D_CONV), 0.02)
    cln_g = 1.0 + nrm(ks[8], (DEPTH, D_CONV), 0.05)
    cln_b = nrm(ks[9], (DEPTH, D_CONV), 0.02)
    pw2_w = nrm(ks[10], (DEPTH, D_CONV, D_CONV), D_CONV ** -0.5)
    pw2_b = nrm(ks[11], (DEPTH, D_CONV), 0.02)
    sc_w = nrm(ks[12], (DEPTH, SHORT_CONV, D_LRU), SHORT_CONV ** -0.5)
    sc_b = nrm(ks[13], (DEPTH, D_LRU), 0.02)
    wr = nrm(ks[14], (DEPTH, LRU_HEADS, LRU_HEAD_DIM, LRU_HEAD_DIM), LRU_HEAD_DIM ** -0.5)
    br = nrm(ks[15], (DEPTH, D_LRU), 0.02)
    wi = nrm(ks[16], (DEPTH, LRU_HEADS, LRU_HEAD_DIM, LRU_HEAD_DIM), LRU_HEAD_DIM ** -0.5)
    bi = nrm(ks[17], (DEPTH, D_LRU), 0.02)
    a0 = jax.random.uniform(ks[18], (DEPTH, D_LRU), f32, 0.9, 0.999)
    s = a0 ** (1.0 / LRU_C)
    lam = jnp.log(s) - jnp.log1p(-s)
    w_out = nrm(ks[19], (DEPTH, D_MIX, D_MODEL), D_MIX ** -0.5)
    final_g = 1.0 + nrm(ks[20], (D_MODEL,), 0.05)
    return {"x": x, "c": c, "norm_g": norm_g, "mod_w": mod_w, "mod_b": mod_b,
            "w_in": w_in, "dw_w": dw_w, "dw_b": dw_b, "cln_g": cln_g, "cln_b": cln_b,
            "pw2_w": pw2_w, "pw2_b": pw2_b, "sc_w": sc_w, "sc_b": sc_b,
            "wr": wr, "br": br, "wi": wi, "bi": bi, "lam": lam,
            "w_out": w_out, "final_g": final_g}


def reference(x, c, norm_g, mod_w, mod_b, w_in, dw_w, dw_b, cln_g, cln_b,
              pw2_w, pw2_b, sc_w, sc_b, wr, br, wi, bi, lam, w_out, final_g):
    bsz, seq, _ = x.shape
    c_act = jax.nn.silu(c)
    for l in range(DEPTH):
        mod = c_act @ mod_w[l] + mod_b[l]
        shift, scale, gate = jnp.split(mod, 3, axis=-1)
        h = rmsnorm(x, norm_g[l]) * (1.0 + scale[:, None, :]) + shift[:, None, :]

        u = h @ w_in[l]
        cv, cg, cs, lx, ls = jnp.split(
            u, [D_CONV, 2 * D_CONV, 3 * D_CONV, 3 * D_CONV + D_LRU], axis=-1)

        v = cv * jax.nn.sigmoid(cg)
        v = causal_depthwise_conv(v, dw_w[l], dw_b[l])
        v = jax.nn.silu(layernorm(v, cln_g[l], cln_b[l]))
        v = v @ pw2_w[l] + pw2_b[l]
        y_conv = v * jax.nn.silu(cs)

        xs = causal_depthwise_conv(lx, sc_w[l], sc_b[l])
        xh = xs.reshape(bsz, seq, LRU_HEADS, LRU_HEAD_DIM)
        r = jax.nn.sigmoid(jnp.einsum("bshd,hde->bshe", xh, wr[l]).reshape(bsz, seq, D_LRU) + br[l])
        i = jax.nn.sigmoid(jnp.einsum("bshd,hde->bshe", xh, wi[l]).reshape(bsz, seq, D_LRU) + bi[l])
        log_a = (-LRU_C * r.astype(jnp.float32)
                 * jax.nn.softplus(-lam[l].astype(jnp.float32)))
        a = jnp.exp(log_a)
        mult = jnp.sqrt(-jnp.expm1(2.0 * log_a))
        bx = mult * (i * xs).astype(jnp.float32)
        hs = linear_scan(a, bx).astype(x.dtype)
        y_lru = hs * jax.nn.silu(ls)

        y = jnp.concatenate([y_conv, y_lru], axis=-1) @ w_out[l]
        x = x + (1.0 + gate[:, None, :]) * y
    return rmsnorm(x, final_g)
```

```python
import numpy as np
import concourse.bass as bass
import concourse.mybir as mybir
from concourse.bass_utils import run_bass_kernel_spmd

F32 = mybir.dt.float32
BF16 = mybir.dt.bfloat16
AF = mybir.ActivationFunctionType
ALU = mybir.AluOpType

NCORE = 8
B, S, D = 16, 4096, 1024
NB = B // NCORE
T = 512
NT = S // T
DEPTH = 2
NCH = 8
KCONV = 31
KSC = 4
HALO = KCONV - 1
HALO2 = KSC - 1
VW = T + HALO + 2
LW = T + HALO2
EPS = 1e-6
NS = 4
PPL = 16

LP = 408
NG4 = 8
W4 = T + 28
OFF = {'dw_b': 0, 'cln_g': 8, 'cln_b': 16, 'pw2_b': 24, 'sc_b': 32, 'br': 40, 'bi': 48,
       'lam': 56, 'norm_g': 64, 'dww': 72, 'scw': 328, 'mod_b': 360}
OFF_FINAL = DEPTH * LP
OFF_IDENT = OFF_FINAL + 8
OFF_IDB = OFF_IDENT + 128
NP = OFF_IDB + 32

PIECES = ['g0', 'g2', 'g1', 'g3', 'g6', 'g7', 'g8', 'g9', 'g4', 'g5', 'P0', 'P1', 'WO0', 'WO1', 'WO2', 'WO3']


class Reg:
    __slots__ = ('name', 'writer', 'readers')

    def __init__(self, name):
        self.name = name
        self.writer = None
        self.readers = {}


class Prog:
    ENGS = ['pe', 'act', 'dve', 'pool', 'sp']

    def __init__(self, nc, sems, junk):
        self.nc = nc
        self.q = {e: [] for e in self.ENGS}
        self.sem = sems
        self.cnt = {}
        self.ack = {e: {} for e in self.ENGS}
        self.junk = junk
        self.jk = 0
        self.nops = {e: 0 for e in self.ENGS}
        self.cur = {e: e for e in self.ENGS}
        self.eng_of = {e: e for e in self.ENGS}

    def new_epoch(self, E, key, semh):
        self.sem[key] = semh
        self.cur[E] = key
        self.eng_of[key] = E

    def _deps(self, reads, writes):
        deps = {}
        for r in reads:
            if r.writer is not None:
                k, n = r.writer
                if n > deps.get(k, 0):
                    deps[k] = n
        for w in writes:
            if w.writer is not None:
                k, n = w.writer
                if n > deps.get(k, 0):
                    deps[k] = n
            for k, n in w.readers.items():
                if n > deps.get(k, 0):
                    deps[k] = n
        return deps

    def _need(self, E, deps):
        for X, n in deps.items():
            if self.eng_of.get(X) == E and (E == 'pe' or n > self.cnt.get(X, 0)):
                continue
            if n > self.ack[E].get(X, 0):
                self.ack[E][X] = n
                sem = self.sem[X]
                self.q[E].append(lambda eng, sem=sem, v=n: eng.wait_ge(sem, v))

    def _mark(self, tag, reads, writes):
        k, n = tag
        for w in writes:
            w.writer = tag
            w.readers = {}
        for r in reads:
            if n > r.readers.get(k, 0):
                r.readers[k] = n

    def dma(self, E, dsem, out, in_, reads=(), writes=()):
        self._need(E, self._deps(reads, writes))
        self.cnt[dsem] = self.cnt.get(dsem, 0) + 16
        tag = (dsem, self.cnt[dsem])
        sem = self.sem[dsem]
        self.q[E].append(lambda eng, sem=sem, out=out, in_=in_: eng.dma_start(out=out, in_=in_).then_inc(sem, 16))
        self._mark(tag, reads, writes)
        self.nops[E] += 1

    def op(self, E, fn, reads=(), writes=(), inc=True):
        self._need(E, self._deps(reads, writes))
        key = self.cur[E]
        tag = (key, self.cnt.get(key, 0) + 1)
        if inc:
            self.cnt[key] = tag[1]
            sem = self.sem[key]
            self.q[E].append(lambda eng, fn=fn, sem=sem: fn(eng).then_inc(sem, 1))
        else:
            self.q[E].append(lambda eng, fn=fn: fn(eng))
        self._mark(tag, reads, writes)
        self.nops[E] += 1

    def barrier(self):
        allk = {k: v for k, v in self.cnt.items()}
        for E in self.ENGS:
            self._need(E, allk)

    def run(self):
        nc = self.nc
        with nc.Block() as block:
            @block.tensor
            def _(e):
                for f in self.q['pe']:
                    f(e)

            @block.scalar
            def _(e):
                for f in self.q['act']:
                    f(e)

            @block.vector
            def _(e):
                for f in self.q['dve']:
                    f(e)

            @block.gpsimd
            def _(e):
                for f in self.q['pool']:
                    f(e)

            @block.sync
            def _(e):
                for f in self.q['sp']:
                    f(e)


def ap3(ap2, a, b, sa=None, sb=1):
    if sa is None:
        sa = b
    return bass.AP(ap2.tensor, ap2.offset, [list(ap2.ap[0]), [sa, a], [sb, b]])


class Rot:
    def __init__(self, tiles, name):
        self.tiles = tiles
        self.regs = [Reg(f"{name}{i}") for i in range(len(tiles))]
        self.i = 0

    def next(self):
        i = self.i % len(self.tiles)
        self.i += 1
        return self.tiles[i], self.regs[i]


class _Stop(Exception):
    pass


def build_nc(NB=NB, NT=NT, DEPTH=DEPTH, stop=None, dump=None):
    S = NT * T
    nc = bass.Bass("TRN2", target_bir_lowering=False)
    xT = nc.dram_tensor("xT", [NB, D, S], F32, kind="ExternalInput").ap()
    cT = nc.dram_tensor("cT", [128, 2 * NCH], F32, kind="ExternalInput").ap()
    prm_d = nc.dram_tensor("prm", [128, NP], F32, kind="ExternalInput").ap()
    w_in_d = nc.dram_tensor("w_in", [DEPTH, D, 5 * D], F32, kind="ExternalInput").ap()
    pw2_d = nc.dram_tensor("pw2_w", [DEPTH, D, D], F32, kind="ExternalInput").ap()
    w_out_d = nc.dram_tensor("w_out", [DEPTH, 2 * D, D], F32, kind="ExternalInput").ap()
    wr_d = nc.dram_tensor("wr", [DEPTH, 8, 128, 128], F32, kind="ExternalInput").ap()
    wi_d = nc.dram_tensor("wi", [DEPTH, 8, 128, 128], F32, kind="ExternalInput").ap()
    mod_w_d = nc.dram_tensor("mod_w", [DEPTH, D, 3 * D], F32, kind="ExternalInput").ap()
    yT = nc.dram_tensor("yT", [NB, D, S], F32, kind="ExternalOutput").ap()
    wsc = nc.dram_tensor("wsc", [DEPTH * PPL, 128, 4096], BF16, kind="Internal").ap()
    vscr = nc.dram_tensor("vscr", [3, 128, VW], BF16, kind="Internal").ap()

    from contextlib import ExitStack
    with ExitStack() as st:
        def sb(name, shape, dt):
            return st.enter_context(nc.sbuf_tensor("sb_" + name, shape, dt))

        def sem(name):
            return st.enter_context(nc.semaphore(name))

        prm = sb("prm", [128, NP], F32)
        xres = sb("xres", [128, NCH * T], F32)
        import os
        xres2 = sb("xres2", [128, NCH * T], F32) if os.environ.get('DBG_XDST2') else None
        vc = sb("vc", [128, NCH * T], F32)
        stat = [sb(f"stat{i}", [128, T], F32) for i in range(6)]
        tf_t = [sb(f"tf{i}", [128, T], F32) for i in range(14)]
        modsb = sb("modsb", [128, DEPTH * 48], F32)
        Amod = sb("Amod", [128, DEPTH * NB * NCH], F32)
        Gmod = sb("Gmod", [128, DEPTH * NB * NCH], F32)
        Kc = sb("Kc", [128, DEPTH * NCH], F32)
        K2 = sb("K2", [128, DEPTH * NCH], F32)
        ktmp = sb("ktmp", [128, DEPTH * NCH], F32)
        Kh = sb("Kh", [128, DEPTH * NCH], F32)
        hb = sb("hb", [128, DEPTH * 2 * NCH], F32)
        cact = sb("cact", [128, 2 * NCH], F32)
        hstate = sb("hstate", [128, DEPTH * NCH], F32)
        cst = sb("cst", [128, 4], F32)

        h = sb("h", [128, NCH * T], BF16)
        v = sb("v", [128, NCH * VW], BF16)
        vtail = sb("vtail", [128, DEPTH * NCH * HALO], BF16)
        vn = sb("vn", [128, NCH * T], BF16)
        yconv = sb("yconv", [128, NCH * T], BF16)
        ylru = sb("ylru", [128, NCH * T], BF16)
        lxb = sb("lxb", [128, NCH * LW], BF16)
        ltail = sb("ltail", [128, DEPTH * NCH * HALO2], BF16)
        xsb = sb("xsb", [128, NCH * T], BF16)
        scs_all = sb("scs_all", [128, NCH * T], BF16)
        ident = sb("ident", [128, 128], BF16)
        onesm = sb("onesm", [128, 128], BF16)
        dg_t = [sb(f"dg{i}", [128, NG4 * 128], BF16) for i in range(2)]
        identb = sb("identb", [128, 32], BF16)
        v4_t = [sb(f"v4_{i}", [128, 4 * W4], BF16) for i in range(3)]
        sdg_t = [sb(f"sdg{i}", [128, KSC * 128], BF16) for i in range(2)]
        rotb_t = [sb(f"rotb{i}", [128, T], BF16) for i in range(6)]
        gates = sb("gates", [128, DEPTH * 2048], BF16)
        ring = [sb(f"ring{i}", [128, 4096], BF16) for i in range(NS)]

        ps = [st.enter_context(nc.psum_tensor(f"ps{i}", [128, T], F32)) for i in range(8)]

        sems = {e: sem(f"s_{e}") for e in ['pe', 'act', 'dve', 'pool']}
        junk = [sem(f"junk{i}") for i in range(4)]
        P = Prog(nc, sems, junk)

        def dsem(name):
            P.sem[name] = sem("d_" + name)
            return name

        ep_sems = [{e: sem(f"s_{e}_ep{i}") for e in ['pe', 'act', 'dve', 'pool']} for i in range(NB * NT)]
        d_misc = dsem("misc")
        d_misc2 = dsem("misc2")
        d_stg = [dsem(f"stg{i}") for i in range(2)]
        d_sto = [dsem(f"sto{i}") for i in range(NS)]
        d_ring = [dsem(f"ring{i}") for i in range(NS)]
        d_v4 = [dsem(f"v4_{i}") for i in range(3)]
        d_v4w = [dsem(f"v4w_{i}") for i in range(3)]
        r_vscr = [Reg(f"vscr{i}") for i in range(3)]
        r_identb = Reg("identb")
        d_x = dsem("x")
        d_y = dsem("y")

        r_prm = Reg("prm")
        r_x = [Reg(f"x{c}") for c in range(NCH)]
        r_vc = [Reg(f"vc{c}") for c in range(NCH)]
        r_stat = [Reg(f"stat{i}") for i in range(6)]
        r_mod = Reg("mod")
        r_K = Reg("K")
        r_cact = Reg("cact")
        r_hst = [[Reg(f"hst{l}_{c}") for c in range(NCH)] for l in range(DEPTH)]
        r_cst = Reg("cst")
        r_h = [Reg(f"h{c}") for c in range(NCH)]
        r_v = [Reg(f"v{c}") for c in range(NCH)]
        r_vtail = [Reg(f"vtail{l}") for l in range(DEPTH)]
        r_vn = [Reg(f"vn{c}") for c in range(NCH)]
        r_yc = [Reg(f"yc{c}") for c in range(NCH)]
        r_yl = [Reg(f"yl{c}") for c in range(NCH)]
        r_lxb = [Reg(f"lxb{c}") for c in range(NCH)]
        r_ltail = [Reg(f"ltail{l}") for l in range(DEPTH)]
        r_xsb = [Reg(f"xsb{c}") for c in range(NCH)]
        r_scs = [Reg(f"scs{c}") for c in range(NCH)]
        r_ident = Reg("ident")
        r_ones = Reg("ones")
        r_gates = Reg("gates")
        r_ring = [Reg(f"ring{i}") for i in range(NS)]
        r_wsc = [Reg(f"wsc{i}") for i in range(DEPTH * PPL)]
        r_ps = [Reg(f"ps{i}") for i in range(8)]

        tf = Rot([t[:, :] for t in tf_t], "tf")
        rotb = Rot([t[:, :] for t in rotb_t], "rotb")
        dgr = Rot([t[:, :] for t in dg_t], "dg")
        sdgr = Rot([t[:, :] for t in sdg_t], "sdg")
        psA = Rot([ps[i][:, :] for i in range(4)], "psA")
        psA.regs = r_ps[0:4]
        psC = Rot([ps[i][:, :] for i in (4, 5)], "psC")
        psC.regs = r_ps[4:6]
        psS = [ps[6][:, :], ps[7][:, :]]
        r_psS = r_ps[6:8]

        def ch(t, c, w=T):
            return t[:, c * w:(c + 1) * w]

        def pcol(l, name, c):
            o = l * LP + OFF[name] + c
            return prm[:, o:o + 1]

        _stg = ['p0', 'p1', 'p2', 'p3']
        lvl = _stg.index(stop) if stop in _stg else 99
        P.dma('sp', d_misc, prm[:, :], prm_d, writes=[r_prm])
        P.dma('sp', d_misc2, cact[:, :], cT, writes=[r_cact])
        P.op('pool', lambda e: e.memset(cst[:, 0:1], EPS), writes=[r_cst])
        P.op('pool', lambda e: e.memset(cst[:, 1:2], 1.0), writes=[r_cst])
        P.op('pool', lambda e: e.memset(cst[:, 2:3], -0.6931471805599453), writes=[r_cst])
        P.op('pool', lambda e: e.memset(onesm[:, :], 1.0 / D), writes=[r_ones])
        P.op('pool', lambda e: e.memset(hstate[:, :], 0.0), writes=[x for l in r_hst for x in l])
        P.op('dve', lambda e: e.tensor_copy(out=ident[:, :], in_=prm[:, OFF_IDENT:OFF_IDENT + 128]),
             reads=[r_prm], writes=[r_ident])
        P.op('dve', lambda e: e.tensor_copy(out=identb[:, :], in_=prm[:, OFF_IDB:OFF_IDB + 32]),
             reads=[r_prm], writes=[r_identb])
        P.op('pool', lambda e: e.memset(v[:, :], 0.0), writes=r_v)
        v4r = Rot([t for t in v4_t], "v4")
        for _i in range(3):
            P.op('pool', lambda e, _i=_i: e.memset(v4_t[_i][:, :], 0.0), writes=[v4r.regs[_i]])
        P.op('act', lambda e: e.activation(out=cact[:, :], in_=cact[:, :], func=AF.Silu), reads=[r_cact], writes=[r_cact])
        for l in range(DEPTH):
            lam_ap = prm[:, l * LP + OFF['lam']: l * LP + OFF['lam'] + NCH]
            kt = ktmp[:, l * NCH:(l + 1) * NCH]
            P.op('act', lambda e, lam_ap=lam_ap, kt=kt: e.activation(out=kt, in_=lam_ap, func=AF.Exp, scale=-1.0),
                 reads=[r_prm], writes=[r_K])
            P.op('act', lambda e, kt=kt: e.activation(out=kt, in_=kt, func=AF.Ln, bias=cst[:, 1:2]),
                 reads=[r_K, r_cst], writes=[r_K])
            P.op('dve', lambda e, kt=kt, l=l: e.tensor_scalar(out=Kc[:, l * NCH:(l + 1) * NCH], in0=kt, scalar1=-8.0,
                                                           scalar2=None, op0=ALU.mult), reads=[r_K], writes=[r_K])
            P.op('dve', lambda e, kt=kt, l=l: e.tensor_scalar(out=K2[:, l * NCH:(l + 1) * NCH], in0=kt, scalar1=-16.0,
                                                           scalar2=None, op0=ALU.mult), reads=[r_K], writes=[r_K])
            P.op('dve', lambda e, kt=kt, l=l: e.tensor_scalar(out=Kh[:, l * NCH:(l + 1) * NCH], in0=kt, scalar1=-4.0,
                                                           scalar2=None, op0=ALU.mult), reads=[r_K], writes=[r_K])
            P.op('dve', lambda e, l=l: e.tensor_scalar(out=hb[:, l * 2 * NCH:(l + 1) * 2 * NCH],
                                                     in0=prm[:, l * LP + OFF['br']: l * LP + OFF['br'] + 2 * NCH], scalar1=0.5,
                                                     scalar2=None, op0=ALU.mult), reads=[r_prm], writes=[r_K])

        stg = [xres[:, :], vc[:, :]]
        r_stg = [Reg("stg0"), Reg("stg1")]
        stg_i = [0]

        def next_stage():
            i = stg_i[0] % 2
            stg_i[0] += 1
            return i

        for l in (range(DEPTH) if lvl >= 1 else []):
            mps = ps[0][:, 0:48]
            for g in range(6):
                si = next_stage()
                src = mod_w_d[l].rearrange("(kc p) n -> p kc n", p=128)[:, :, g * 512:(g + 1) * 512]
                P.dma('sp', d_stg[si], ap3(stg[si], NCH, 512), src, writes=[r_stg[si]])
                import os
                for m in (range(4) if not os.environ.get('DBG_NOMODMM') else []):
                    oc = g * 4 + m
                    for kc in range(NCH):
                        lhsT = stg[si][:, kc * 512 + m * 128: kc * 512 + (m + 1) * 128]
                        rhs = cact[:, kc * 2:(kc + 1) * 2]
                        P.op('pe', lambda e, lhsT=lhsT, rhs=rhs, kc=kc, oc=oc: e.matmul(
                            ps[0][:, oc * 2:(oc + 1) * 2], lhsT, rhs, start=(kc == 0), stop=(kc == NCH - 1)),
                            reads=[r_stg[si], r_cact], writes=[r_ps[0]], inc=True)
            mb = prm[:, l * LP + OFF['mod_b']: l * LP + OFF['mod_b'] + 48]
            P.op('dve', lambda e, l=l, mb=mb, mps=mps: e.tensor_tensor(out=modsb[:, l * 48:(l + 1) * 48], in0=mps, in1=mb,
                                                                   op=ALU.add), reads=[r_ps[0], r_prm], writes=[r_mod])
            for b in range(NB):
                base = modsb[:, l * 48 + 16 + b: l * 48 + 16 + b + 1]
                scale_v = bass.AP(base.tensor, base.offset, [list(base.ap[0]), [2, NCH]])
                base2 = modsb[:, l * 48 + 32 + b: l * 48 + 32 + b + 1]
                gate_v = bass.AP(base2.tensor, base2.offset, [list(base2.ap[0]), [2, NCH]])
                o = (l * NB + b) * NCH
                ng = prm[:, l * LP + OFF['norm_g']: l * LP + OFF['norm_g'] + NCH]
                P.op('dve', lambda e, o=o, scale_v=scale_v, ng=ng: e.scalar_tensor_tensor(
                    out=Amod[:, o:o + NCH], in0=scale_v, scalar=1.0, in1=ng, op0=ALU.add, op1=ALU.mult),
                    reads=[r_mod, r_prm], writes=[r_mod])
                P.op('dve', lambda e, o=o, gate_v=gate_v: e.tensor_scalar(
                    out=Gmod[:, o:o + NCH], in0=gate_v, scalar1=1.0, scalar2=None, op0=ALU.add),
                    reads=[r_mod], writes=[r_mod])

        def shift_col(l, b, c):
            o = l * 48 + c * 2 + b
            return modsb[:, o:o + 1]

        for l in (range(DEPTH) if lvl >= 2 else []):
            si = next_stage()
            P.dma('sp', d_stg[si], ap3(stg[si][:, 0:1024], 8, 128), wr_d[l].rearrange("h d e -> d h e"), writes=[r_stg[si]])
            P.dma('sp', d_stg[si], ap3(stg[si][:, 1024:2048], 8, 128), wi_d[l].rearrange("h d e -> d h e"), writes=[r_stg[si]])
            P.op('dve', lambda e, l=l, si=si: e.tensor_copy(out=gates[:, l * 2048:(l + 1) * 2048], in_=stg[si][:, 0:2048]),
                 reads=[r_stg[si]], writes=[r_gates])

        def piece_src(l, name):
            if name[0] == 'g':
                g = int(name[1:])
                return w_in_d[l].rearrange("(kc p) n -> p kc n", p=128)[:, :, g * 512:(g + 1) * 512], (NCH, 512)
            if name[0] == 'P':
                q = int(name[1:])
                return pw2_d[l].rearrange("(kc p) n -> p kc n", p=128)[:, :, q * 512:(q + 1) * 512], (NCH, 512)
            q = int(name[2:])
            return w_out_d[l].rearrange("(kc p) n -> p kc n", p=128)[:, :, q * 256:(q + 1) * 256], (2 * NCH, 256)

        cast_engs = ['act', 'dve']
        pi = 0
        for l in (range(DEPTH) if lvl >= 3 else []):
            for k, name in enumerate(PIECES):
                pid = l * PPL + k
                si = next_stage()
                src, (a, bb) = piece_src(l, name)
                P.dma('sp', d_stg[si], ap3(stg[si], a, bb), src, writes=[r_stg[si]])
                rs = pi % NS
                E = cast_engs[pi % 2]
                if E == 'act':
                    P.op('act', lambda e, rs=rs, si=si: e.activation(out=ring[rs][:, :], in_=stg[si], func=AF.Copy),
                         reads=[r_stg[si]], writes=[r_ring[rs]])
                else:
                    P.op(E, lambda e, rs=rs, si=si: e.tensor_copy(out=ring[rs][:, :], in_=stg[si]),
                         reads=[r_stg[si]], writes=[r_ring[rs]])
                P.dma('sp', d_sto[rs], wsc[pid], ring[rs][:, :], reads=[r_ring[rs]], writes=[r_wsc[pid]])
                pi += 1

        P.barrier()

        stream_len = NB * NT * DEPTH * PPL
        state = {'issued': 0, 'next': 0, 'retired': 0}

        def ring_issue(m):
            tl = m // PPL
            l = tl % DEPTH
            pid = l * PPL + (m % PPL)
            s = m % NS
            P.dma('sp', d_ring[s], ring[s][:, :], wsc[pid], reads=[r_wsc[pid]], writes=[r_ring[s]])

        def ring_fill():
            while state['issued'] < stream_len and state['issued'] < state['retired'] + NS:
                ring_issue(state['issued'])
                state['issued'] += 1

        def ring_acquire():
            m = state['next']
            state['next'] += 1
            assert m < state['issued'], "ring piece not issued before use"
            return m % NS

        def ring_retire(n=1):
            state['retired'] += n
            ring_fill()

        ring_fill()

        def mm(out_ap, out_reg, lhsT, lregs, rhs, rregs, start, stop, tile_position=None):
            P.op('pe', lambda e: e.matmul(out_ap, lhsT, rhs, start=start, stop=stop, tile_position=tile_position),
                 reads=list(lregs) + list(rregs), writes=[out_reg], inc=True)

        def rms_stats(src_t, src_regs, rstd_tile, r_rstd):
            for c in range(NCH):
                sq, r_sq = rotb.next()
                P.op('act', lambda e, c=c, sq=sq: e.activation(out=sq, in_=ch(src_t, c), func=AF.Square),
                     reads=[src_regs[c]], writes=[r_sq])
                mm(psS[0], r_psS[0], onesm[:, :], [r_ones], sq, [r_sq], c == 0, c == NCH - 1)
            sd, r_sd = tf.next()
            P.op('act', lambda e: e.activation(out=sd, in_=psS[0], func=AF.Ln, bias=cst[:, 0:1]),
                 reads=[r_psS[0], r_cst], writes=[r_sd])
            P.op('act', lambda e: e.activation(out=rstd_tile, in_=sd, func=AF.Exp, scale=-0.5), reads=[r_sd], writes=[r_rstd])

        tl_total = NB * NT * DEPTH
        dg_fifo, sdg_fifo = [], []
        gen_state = {'dg': 0, 'sdg': 0}

        def gen_next_dg():
            n = gen_state['dg']
            if n >= tl_total * NCH:
                return
            gen_state['dg'] = n + 1
            l_ = (n // NCH) % DEPTH
            c_ = n % NCH
            dg, r_dg = dgr.next()
            dwo = l_ * LP + OFF['dww'] + c_ * NG4 * 4
            dwb = prm[:, dwo:dwo + 1]
            ib = identb[:, :]
            in0 = bass.AP(ib.tensor, ib.offset, [list(ib.ap[0]), [0, NG4 * 4], [1, 32]])
            in1 = bass.AP(dwb.tensor, dwb.offset, [list(dwb.ap[0]), [1, NG4 * 4], [0, 32]])
            P.op('pool', lambda e, dg=dg, in0=in0, in1=in1: e.tensor_tensor(
                out=ap3(dg, NG4 * 4, 32), in0=in0, in1=in1, op=ALU.mult),
                reads=[r_identb, r_prm], writes=[r_dg])
            dg_fifo.append((dg, r_dg))

        v4_fifo = []
        import os as _os
        V4Q = _os.environ.get('DBG_V4Q', 'sp')

        def build_v4(c_):
            sl = v4r.i % 3
            vt, r_vt = v4r.next()
            P.dma(V4Q, d_v4w[sl], vscr[sl], v[:, c_ * VW:(c_ + 1) * VW], reads=[r_v[c_]], writes=[r_vscr[sl]])
            sa = vscr[sl]
            for ii in range(4):
                src = bass.AP(sa.tensor, sa.offset + ii, [[VW, 32], [32 * VW, 4], [1, W4]])
                dd = vt[32 * ii:32 * (ii + 1), :]
                dst = bass.AP(dd.tensor, dd.offset, [list(dd.ap[0]), [W4, 4], [1, W4]])
                P.dma(V4Q, d_v4[sl], dst, src, reads=[r_vscr[sl]], writes=[r_vt])
            v4_fifo.append((vt, r_vt))

        def gen_next_sdg():
            n = gen_state['sdg']
            if n >= tl_total * NCH:
                return
            gen_state['sdg'] = n + 1
            l_ = (n // NCH) % DEPTH
            c_ = n % NCH
            sdg, r_sdg = sdgr.next()
            swo = l_ * LP + OFF['scw'] + c_ * KSC
            swb = prm[:, swo:swo + 1]
            in0 = bass.AP(ident[:, :].tensor, ident[:, :].offset, [list(ident[:, :].ap[0]), [0, KSC], [1, 128]])
            in1 = bass.AP(swb.tensor, swb.offset, [list(swb.ap[0]), [1, KSC], [0, 128]])
            P.op('pool', lambda e, sdg=sdg, in0=in0, in1=in1: e.tensor_tensor(
                out=ap3(sdg, KSC, 128), in0=in0, in1=in1, op=ALU.mult), reads=[r_ident, r_prm], writes=[r_sdg])
            sdg_fifo.append((sdg, r_sdg))

        for _ in range(2):
            gen_next_dg()
            gen_next_sdg()

        pending_out = []

        def flush_out():
            while pending_out:
                dst_, src_, regs_ = pending_out.pop(0)
                P.dma('sp', d_y, dst_, src_, reads=regs_)

        bufs2 = [(xres, r_x), (vc, r_vc)]
        total_tiles = NB * NT

        def load_x(tt_):
            Xb, r_Xb = bufs2[tt_ % 2]
            b_, j_ = tt_ // NT, tt_ % NT
            P.dma('sp', d_x, ap3(Xb[:, :], NCH, T), xT[b_].rearrange("(c p) s -> p c s", p=128)[:, :, j_ * T:(j_ + 1) * T],
                  writes=r_Xb)

        def phase_A(tt_, l_):
            Xb, r_Xb = bufs2[tt_ % 2]
            b_ = tt_ // NT
            rstd_x = stat[0][:, :]
            rms_stats(Xb, r_Xb, rstd_x, r_stat[0])
            for c in range(NCH):
                t1, r_t1 = tf.next()
                acol = Amod[:, (l_ * NB + b_) * NCH + c:(l_ * NB + b_) * NCH + c + 1]
                P.op('dve', lambda e, c=c, t1=t1, acol=acol, Xb=Xb: e.scalar_tensor_tensor(
                    out=t1, in0=ch(Xb, c), scalar=acol, in1=rstd_x, op0=ALU.mult, op1=ALU.mult),
                    reads=[r_Xb[c], r_mod, r_stat[0]], writes=[r_t1])
                scol = shift_col(l_, b_, c)
                P.op('act', lambda e, c=c, t1=t1, scol=scol: e.activation(
                    out=ch(h, c), in_=t1, func=AF.Identity, bias=scol),
                    reads=[r_t1, r_mod], writes=[r_h[c]])

        _occ = {}

        def chk(tag):
            _occ[tag] = _occ.get(tag, 0) + 1
            if stop == tag or (tag == 'prologue' and stop in _stg) or stop == f"{tag}@{_occ[tag] - 1}":
                raise _Stop()

        try:
            chk('prologue')
            for b in range(NB):
                for j in range(NT):
                    first = (j == 0)
                    for _e in ['pe', 'act', 'dve', 'pool']:
                        P.new_epoch(_e, f"{_e}#{b * NT + j}", ep_sems[b * NT + j][_e])
                    cols = slice(j * T, (j + 1) * T)
                    tt = b * NT + j
                    X, r_X = bufs2[tt % 2]
                    V, r_V = bufs2[(tt + 1) % 2]
                    if tt == 0:
                        load_x(0)
                        phase_A(0, 0)
                    chk('X')
                    for l in range(DEPTH):
                        if l > 0:
                            phase_A(tt, l)
                        chk('A')
                        vh = ap3(v[:, :], NCH, HALO, sa=VW)
                        lh = ap3(lxb[:, :], NCH, HALO2, sa=LW)
                        if first:
                            P.op('pool', lambda e, vh=vh: e.memset(vh, 0.0), writes=r_v)
                            P.op('pool', lambda e, lh=lh: e.memset(lh, 0.0), writes=r_lxb)
                        else:
                            vt = ap3(vtail[:, l * NCH * HALO:(l + 1) * NCH * HALO], NCH, HALO)
                            lt = ap3(ltail[:, l * NCH * HALO2:(l + 1) * NCH * HALO2], NCH, HALO2)
                            P.op('pool', lambda e, vh=vh, vt=vt: e.tensor_copy(out=vh, in_=vt), reads=[r_vtail[l]], writes=r_v)
                            P.op('pool', lambda e, lh=lh, lt=lt: e.tensor_copy(out=lh, in_=lt), reads=[r_ltail[l]], writes=r_lxb)

                        for half in range(2):
                            s_cv = ring_acquire()
                            s_cg = ring_acquire()
                            for m in range(4):
                                c = half * 4 + m
                                pa, r_pa = psA.next()
                                pb, r_pb = psA.next()
                                for kc in range(NCH):
                                    mm(pa, r_pa, ring[s_cv][:, kc * 512 + m * 128: kc * 512 + (m + 1) * 128], [r_ring[s_cv]],
                                       ch(h, kc), [r_h[kc]], kc == 0, kc == NCH - 1)
                                for kc in range(NCH):
                                    mm(pb, r_pb, ring[s_cg][:, kc * 512 + m * 128: kc * 512 + (m + 1) * 128], [r_ring[s_cg]],
                                       ch(h, kc), [r_h[kc]], kc == 0, kc == NCH - 1)
                                sg, r_sg = tf.next()
                                P.op('act', lambda e, sg=sg, pb=pb: e.activation(out=sg, in_=pb, func=AF.Sigmoid),
                                     reads=[r_pb], writes=[r_sg])
                                vdst = v[:, c * VW + HALO: c * VW + HALO + T]
                                P.op('dve', lambda e, vdst=vdst, pa=pa, sg=sg: e.tensor_tensor(out=vdst, in0=pa, in1=sg, op=ALU.mult),
                                     reads=[r_pa, r_sg], writes=[r_v[c]])
                                if c < 3:
                                    build_v4(c)
                            ring_retire(2)
                        vt = ap3(vtail[:, l * NCH * HALO:(l + 1) * NCH * HALO], NCH, HALO)
                        vlast = bass.AP(v[:, :].tensor, v[:, T:T + 1].offset, [list(v[:, :].ap[0]), [VW, NCH], [1, HALO]])
                        P.op('pool', lambda e, vt=vt, vlast=vlast: e.tensor_copy(out=vt, in_=vlast), reads=r_v, writes=[r_vtail[l]])

                        chk('B')
                        for half in range(2):
                            s_lx = ring_acquire()
                            for m in range(4):
                                c = half * 4 + m
                                pa, r_pa = psA.next()
                                for kc in range(NCH):
                                    mm(pa, r_pa, ring[s_lx][:, kc * 512 + m * 128: kc * 512 + (m + 1) * 128], [r_ring[s_lx]],
                                       ch(h, kc), [r_h[kc]], kc == 0, kc == NCH - 1)
                                ldst = lxb[:, c * LW + HALO2: c * LW + HALO2 + T]
                                P.op('act', lambda e, ldst=ldst, pa=pa: e.activation(out=ldst, in_=pa, func=AF.Copy),
                                     reads=[r_pa], writes=[r_lxb[c]])
                            ring_retire(1)
                        lt = ap3(ltail[:, l * NCH * HALO2:(l + 1) * NCH * HALO2], NCH, HALO2)
                        llast = bass.AP(lxb[:, :].tensor, lxb[:, T:T + 1].offset, [list(lxb[:, :].ap[0]), [LW, NCH], [1, HALO2]])
                        P.op('pool', lambda e, lt=lt, llast=llast: e.tensor_copy(out=lt, in_=llast), reads=r_lxb, writes=[r_ltail[l]])

                        if l == 0:
                            flush_out()
                        chk('E1')
                        s_ls_state = {'s': None}

                        def lru_chunk(c):
                            s_ls = s_ls_state['s']
                            sdg, r_sdg = sdg_fifo.pop(0)
                            pc, r_pc = psC.next()
                            for k in range(KSC):
                                mm(pc, r_pc, sdg[:, k * 128:(k + 1) * 128], [r_sdg],
                                   lxb[:, c * LW + k: c * LW + k + T], [r_lxb[c]], k == 0, k == KSC - 1)
                            gen_next_sdg()
                            scb = pcol(l, 'sc_b', c)
                            P.op('dve', lambda e, c=c, pc=pc, scb=scb: e.tensor_scalar(out=ch(xsb, c), in0=pc, scalar1=scb, scalar2=None, op0=ALU.add),
                                 reads=[r_pc, r_prm], writes=[r_xsb[c]])
                            if c % 4 == 0:
                                s_ls = ring_acquire()
                            m = c % 4
                            pl_, r_pl = psA.next()
                            for kc in range(NCH):
                                mm(pl_, r_pl, ring[s_ls][:, kc * 512 + m * 128: kc * 512 + (m + 1) * 128], [r_ring[s_ls]],
                                   ch(h, kc), [r_h[kc]], kc == 0, kc == NCH - 1)
                            if c % 4 == 3:
                                ring_retire(1)
                            pr_, r_pr = psA.next()
                            pi_, r_pi = psA.next()
                            go = l * 2048
                            mm(pr_, r_pr, gates[:, go + c * 128: go + (c + 1) * 128], [r_gates], ch(xsb, c), [r_xsb[c]], True, True)
                            mm(pi_, r_pi, gates[:, go + 1024 + c * 128: go + 1024 + (c + 1) * 128], [r_gates], ch(xsb, c), [r_xsb[c]], True, True)
                            sls, r_sls = tf.next()
                            P.op('act', lambda e, sls=sls, pl_=pl_: e.activation(out=sls, in_=pl_, func=AF.Silu),
                                 reads=[r_pl], writes=[r_sls])
                            rr, r_rr = tf.next()
                            ig, r_ig = tf.next()
                            hbr = hb[:, l * 2 * NCH + c: l * 2 * NCH + c + 1]
                            hbi = hb[:, l * 2 * NCH + NCH + c: l * 2 * NCH + NCH + c + 1]
                            P.op('act', lambda e, rr=rr, pr_=pr_, hbr=hbr: e.activation(out=rr, in_=pr_, func=AF.Tanh, scale=0.5, bias=hbr),
                                 reads=[r_pr, r_K], writes=[r_rr])
                            P.op('act', lambda e, ig=ig, pi_=pi_, hbi=hbi: e.activation(out=ig, in_=pi_, func=AF.Tanh, scale=0.5, bias=hbi),
                                 reads=[r_pi, r_K], writes=[r_ig])
                            aa, r_aa = tf.next()
                            a2, r_a2 = tf.next()
                            kcol = Kc[:, l * NCH + c: l * NCH + c + 1]
                            khcol = Kh[:, l * NCH + c: l * NCH + c + 1]
                            P.op('act', lambda e, aa=aa, rr=rr, khcol=khcol: e.activation(out=aa, in_=rr, func=AF.Exp, scale=khcol, bias=khcol),
                                 reads=[r_rr, r_K], writes=[r_aa])
                            P.op('act', lambda e, a2=a2, rr=rr, kcol=kcol: e.activation(out=a2, in_=rr, func=AF.Exp, scale=kcol, bias=kcol),
                                 reads=[r_rr, r_K], writes=[r_a2])
                            P.op('act', lambda e, a2=a2: e.activation(out=a2, in_=a2, func=AF.Ln, scale=-1.0, bias=cst[:, 1:2]),
                                 reads=[r_a2, r_cst], writes=[r_a2])
                            P.op('act', lambda e, a2=a2: e.activation(out=a2, in_=a2, func=AF.Exp, scale=0.5, bias=cst[:, 2:3]),
                                 reads=[r_a2, r_cst], writes=[r_a2])
                            s_ls_state['s'] = s_ls
                            return (c, ig, r_ig, a2, r_a2, aa, r_aa, sls, r_sls)

                        def lru_tail(c, ig, r_ig, a2, r_a2, aa, r_aa, sls, r_sls):
                            P.op('dve', lambda e, ig=ig, c=c: e.scalar_tensor_tensor(out=ig, in0=ig, scalar=1.0, in1=ch(xsb, c),
                                                                                  op0=ALU.add, op1=ALU.mult),
                                 reads=[r_ig, r_xsb[c]], writes=[r_ig])
                            P.op('dve', lambda e, ig=ig, a2=a2: e.tensor_tensor(out=ig, in0=ig, in1=a2, op=ALU.mult),
                                 reads=[r_ig, r_a2], writes=[r_ig])
                            hs, r_hs = tf.next()
                            hcol = hstate[:, l * NCH + c: l * NCH + c + 1]
                            if first:
                                P.op('dve', lambda e, hs=hs, aa=aa, ig=ig: e.tensor_tensor_scan(
                                    out=hs, data0=aa, data1=ig, initial=0.0, op0=ALU.mult, op1=ALU.add),
                                    reads=[r_aa, r_ig], writes=[r_hs])
                            else:
                                P.op('dve', lambda e, hs=hs, aa=aa, ig=ig, hcol=hcol: e.tensor_tensor_scan(
                                    out=hs, data0=aa, data1=ig, initial=hcol, op0=ALU.mult, op1=ALU.add),
                                    reads=[r_aa, r_ig, r_hst[l][c]], writes=[r_hs])
                            P.op('pool', lambda e, hs=hs, hcol=hcol: e.tensor_copy(out=hcol, in_=hs[:, T - 1:T]),
                                 reads=[r_hs], writes=[r_hst[l][c]])
                            P.op('dve', lambda e, c=c, hs=hs, sls=sls: e.tensor_tensor(out=ch(ylru, c), in0=hs, in1=sls, op=ALU.mult),
                                 reads=[r_hs, r_sls], writes=[r_yl[c]])

                        pendq = []
                        lru_pending = None
                        for c in range(NCH + 1):
                            if c < NCH:
                                dg, r_dg = dg_fifo.pop(0)
                                vt, r_vt = v4_fifo.pop(0)
                                pci = 4 + (psC.i % 2)
                                pc, r_pc = psC.next()
                                for g in range(NG4):
                                    for jj in range(4):
                                        mm(ps[pci][32 * jj:32 * (jj + 1), :], r_pc, dg[:, (g * 4 + jj) * 32:(g * 4 + jj + 1) * 32], [r_dg],
                                           vt[:, jj * W4 + 4 * g: jj * W4 + 4 * g + T], [r_vt], g == 0, g == NG4 - 1,
                                           tile_position=(0, 32 * jj))
                                gen_next_dg()
                                if c + 3 < NCH:
                                    build_v4(c + 3)
                                bcol = pcol(l, 'dw_b', c)
                                P.op('dve', lambda e, c=c, pc=pc, bcol=bcol, V=V: e.tensor_scalar(out=ch(V, c), in0=pc, scalar1=bcol, scalar2=None, op0=ALU.add),
                                     reads=[r_pc, r_prm], writes=[r_V[c]])
                                sq, r_sq = rotb.next()
                                P.op('dve', lambda e, c=c, sq=sq, V=V: e.tensor_tensor(out=sq, in0=ch(V, c), in1=ch(V, c), op=ALU.mult),
                                     reads=[r_V[c]], writes=[r_sq])
                                vb, r_vb = rotb.next()
                                P.op('dve', lambda e, c=c, vb=vb, V=V: e.tensor_copy(out=vb, in_=ch(V, c)), reads=[r_V[c]], writes=[r_vb])
                                lru_args = lru_chunk(c)
                                if lru_pending is not None:
                                    lru_tail(*lru_pending)
                                lru_pending = lru_args
                            if c < NCH:
                                pendq.append((c, vb, r_vb, sq, r_sq))
                            while pendq and (len(pendq) > 2 or c >= NCH):
                                pcn, pvb, pr_vb, psq, pr_sq = pendq.pop(0)
                                mm(psS[0], r_psS[0], onesm[:, :], [r_ones], pvb, [pr_vb], pcn == 0, pcn == NCH - 1)
                                mm(psS[1], r_psS[1], onesm[:, :], [r_ones], psq, [pr_sq], pcn == 0, pcn == NCH - 1)
                        lru_tail(*lru_pending)
                        mean_sb, var_sb, sd_sb, rstd_sb, mr_sb = [stat[i][:, :] for i in range(1, 6)]
                        P.op('act', lambda e: e.activation(out=mean_sb, in_=psS[0], func=AF.Copy), reads=[r_psS[0]], writes=[r_stat[1]])
                        P.op('dve', lambda e: e.tensor_tensor(out=var_sb, in0=mean_sb, in1=mean_sb, op=ALU.mult),
                             reads=[r_stat[1]], writes=[r_stat[2]])
                        P.op('dve', lambda e: e.tensor_tensor(out=var_sb, in0=psS[1], in1=var_sb, op=ALU.subtract),
                             reads=[r_psS[1], r_stat[2]], writes=[r_stat[2]])
                        P.op('dve', lambda e: e.tensor_scalar(out=var_sb, in0=var_sb, scalar1=0.0, scalar2=None, op0=ALU.max),
                             reads=[r_stat[2]], writes=[r_stat[2]])
                        P.op('act', lambda e: e.activation(out=sd_sb, in_=var_sb, func=AF.Ln, bias=cst[:, 0:1]),
                             reads=[r_stat[2], r_cst], writes=[r_stat[3]])
                        P.op('act', lambda e: e.activation(out=rstd_sb, in_=sd_sb, func=AF.Exp, scale=-0.5), reads=[r_stat[3]], writes=[r_stat[4]])
                        P.op('dve', lambda e: e.scalar_tensor_tensor(out=mr_sb, in0=mean_sb, scalar=-1.0, in1=rstd_sb,
                                                                    op0=ALU.mult, op1=ALU.mult),
                             reads=[r_stat[1], r_stat[4]], writes=[r_stat[5]])

                        chk('C')
                        s_cs = None
                        for c in range(NCH):
                            if c % 4 == 0:
                                s_cs = ring_acquire()
                            m = c % 4
                            pb, r_pb = psA.next()
                            for kc in range(NCH):
                                mm(pb, r_pb, ring[s_cs][:, kc * 512 + m * 128: kc * 512 + (m + 1) * 128], [r_ring[s_cs]],
                                   ch(h, kc), [r_h[kc]], kc == 0, kc == NCH - 1)
                            if c % 4 == 3:
                                ring_retire(1)
                            P.op('act', lambda e, c=c, pb=pb: e.activation(out=ch(scs_all, c), in_=pb, func=AF.Silu),
                                 reads=[r_pb], writes=[r_scs[c]])
                            t1, r_t1 = tf.next()
                            P.op('dve', lambda e, c=c, t1=t1, V=V: e.tensor_tensor(out=t1, in0=ch(V, c), in1=rstd_sb, op=ALU.mult),
                                 reads=[r_V[c], r_stat[4]], writes=[r_t1])
                            P.op('dve', lambda e, t1=t1: e.tensor_tensor(out=t1, in0=t1, in1=mr_sb, op=ALU.add),
                                 reads=[r_t1, r_stat[5]], writes=[r_t1])
                            P.op('act', lambda e, c=c, t1=t1, l=l: e.activation(out=ch(vn, c), in_=t1, func=AF.Silu,
                                                                              scale=pcol(l, 'cln_g', c), bias=pcol(l, 'cln_b', c)),
                                 reads=[r_t1, r_prm], writes=[r_vn[c]])

                        if l == DEPTH - 1 and tt + 1 < total_tiles:
                            load_x(tt + 1)
                        chk('E2')
                        for half in range(2):
                            s_pw = ring_acquire()
                            for m in range(4):
                                c = half * 4 + m
                                pa, r_pa = psA.next()
                                for kc in range(NCH):
                                    mm(pa, r_pa, ring[s_pw][:, kc * 512 + m * 128: kc * 512 + (m + 1) * 128], [r_ring[s_pw]],
                                       ch(vn, kc), [r_vn[kc]], kc == 0, kc == NCH - 1)
                                P.op('dve', lambda e, c=c, pa=pa, l=l: e.scalar_tensor_tensor(
                                    out=ch(yconv, c), in0=pa, scalar=pcol(l, 'pw2_b', c), in1=ch(scs_all, c), op0=ALU.add, op1=ALU.mult),
                                    reads=[r_pa, r_scs[c], r_prm], writes=[r_yc[c]])
                            ring_retire(1)

                        chk('D')
                        for q in range(4):
                            s_wo = ring_acquire()
                            for m in range(2):
                                c = q * 2 + m
                                pa, r_pa = psA.next()
                                for kc in range(2 * NCH):
                                    src_t, src_r = (yconv, r_yc) if kc < NCH else (ylru, r_yl)
                                    mm(pa, r_pa, ring[s_wo][:, kc * 256 + m * 128: kc * 256 + (m + 1) * 128], [r_ring[s_wo]],
                                       ch(src_t, kc % NCH), [src_r[kc % NCH]], kc == 0, kc == 2 * NCH - 1)
                                gcol = Gmod[:, (l * NB + b) * NCH + c:(l * NB + b) * NCH + c + 1]
                                P.op('dve', lambda e, c=c, pa=pa, gcol=gcol, X=X: e.scalar_tensor_tensor(
                                    out=ch(X, c), in0=pa, scalar=gcol, in1=ch(X, c), op0=ALU.mult, op1=ALU.add),
                                    reads=[r_pa, r_mod, r_X[c]], writes=[r_X[c]])
                            ring_retire(1)

                    chk('F')
                    if tt + 1 < total_tiles:
                        phase_A(tt + 1, 0)
                    rstd_x = stat[0][:, :]
                    rms_stats(X, r_X, rstd_x, r_stat[0])
                    for c in range(NCH):
                        fg = prm[:, OFF_FINAL + c: OFF_FINAL + c + 1]
                        P.op('dve', lambda e, c=c, fg=fg, X=X: e.scalar_tensor_tensor(
                            out=ch(X, c), in0=ch(X, c), scalar=fg, in1=rstd_x, op0=ALU.mult, op1=ALU.mult),
                            reads=[r_X[c], r_prm, r_stat[0]], writes=[r_X[c]])
                    pending_out.append((yT[b].rearrange("(c p) s -> p c s", p=128)[:, :, cols], ap3(X[:, :], NCH, T), r_X))
                    chk('N')
            flush_out()

        except _Stop:
            P.barrier()
            if dump == 'nodma':
                raise_flag = True
            else:
                raise_flag = False
            if dump is not None:
                bufs = {'h': h, 'v': v, 'lxb': lxb, 'vn': vn, 'xsb': xsb, 'ylru': ylru, 'yconv': yconv, 'xres': xres,
                        'stat1': stat[1], 'stat4': stat[4], 'stat0': stat[0], 'modsb': modsb, 'Amod': Amod, 'Gmod': Gmod, 'Kc': Kc, 'gates': gates}
                if dump not in ('vc', 'nodma'):
                    src = bufs[dump]
                    w = min(src[:, :].shape[1], NCH * T)
                    P.op('dve', lambda e: e.tensor_copy(out=vc[:, 0:w], in_=src[:, 0:w]), writes=r_vc)
            if not raise_flag:
                P.dma('sp', d_y, yT[0].rearrange("(c p) s -> p c s", p=128)[:, :, 0:T], ap3(vc[:, :], NCH, T), reads=r_vc)
        P.q['sp'].append(lambda e, s=P.sem[d_y], vv=P.cnt[d_y]: e.wait_ge(s, vv))
        P.run()
    return nc


_NC_CACHE = {}


def _vec(vv):
    return np.ascontiguousarray(np.asarray(vv, np.float32).reshape(NCH, 128).T)


def _prep_prm(inp):
    prm = np.zeros((128, NP), np.float32)
    for l in range(DEPTH):
        base = l * LP
        for name in ['dw_b', 'cln_g', 'cln_b', 'pw2_b', 'sc_b', 'br', 'bi', 'lam', 'norm_g']:
            prm[:, base + OFF[name]: base + OFF[name] + NCH] = _vec(inp[name][l])
        wpad = np.zeros((4 * NG4, D), np.float32)
        wpad[:KCONV] = np.asarray(inp['dw_w'][l], np.float32)
        w5 = wpad.reshape(NG4, 4, NCH, 4, 32)
        tab = w5.transpose(1, 4, 2, 0, 3).reshape(128, NCH * NG4 * 4)
        prm[:, base + OFF['dww']: base + OFF['dww'] + NCH * NG4 * 4] = tab
        scw = np.asarray(inp['sc_w'][l], np.float32).reshape(KSC, NCH, 128).transpose(2, 1, 0)
        prm[:, base + OFF['scw']: base + OFF['scw'] + NCH * KSC] = scw.reshape(128, NCH * KSC)
        mb = np.asarray(inp['mod_b'][l], np.float32).reshape(24, 128).T
        prm[:, base + OFF['mod_b']: base + OFF['mod_b'] + 48] = np.repeat(mb, 2, axis=1)
    prm[:, OFF_FINAL:OFF_FINAL + NCH] = _vec(inp['final_g'])
    prm[:, OFF_IDENT:OFF_IDENT + 128] = np.eye(128, dtype=np.float32)
    prm[:, OFF_IDB:OFF_IDB + 32] = np.tile(np.eye(32, dtype=np.float32), (4, 1))
    return prm


def kernel(**inputs):
    inp = {k: np.asarray(v) for k, v in inputs.items()}
    if 'nc' not in _NC_CACHE:
        _NC_CACHE['nc'] = build_nc()
    nc = _NC_CACHE['nc']
    x = inp['x'].astype(np.float32, copy=False)
    c = inp['c'].astype(np.float32, copy=False)
    prm = _prep_prm(inp)
    shared = {
        'prm': prm,
        'w_in': np.ascontiguousarray(inp['w_in'], np.float32),
        'pw2_w': np.ascontiguousarray(inp['pw2_w'], np.float32),
        'w_out': np.ascontiguousarray(inp['w_out'], np.float32),
        'wr': np.ascontiguousarray(inp['wr'], np.float32),
        'wi': np.ascontiguousarray(inp['wi'], np.float32),
        'mod_w': np.ascontiguousarray(inp['mod_w'], np.float32),
    }
    in_maps = []
    for i in range(NCORE):
        xs = x[i * NB:(i + 1) * NB]
        xT = np.ascontiguousarray(xs.transpose(0, 2, 1))
        cs = c[i * NB:(i + 1) * NB]
        cT = np.ascontiguousarray(cs.reshape(NB, NCH, 128).transpose(2, 1, 0).reshape(128, NCH * NB))
        m = dict(shared)
        m['xT'] = xT
        m['cT'] = cT
        in_maps.append(m)
    res = run_bass_kernel_spmd(nc, in_maps, core_ids=list(range(NCORE)))
    out = np.empty((B, S, D), np.float32)
    for i in range(NCORE):
        yT = np.asarray(res.results[i]['yT'])
        out[i * NB:(i + 1) * NB] = yT.transpose(0, 2, 1)
    return out
```

```python
import numpy as np
import concourse.bass as bass
import concourse.mybir as mybir
from concourse.bass_utils import run_bass_kernel_spmd

F32 = mybir.dt.float32
BF16 = mybir.dt.bfloat16
AF = mybir.ActivationFunctionType
ALU = mybir.AluOpType

NCORE = 8
B, S, D = 16, 4096, 1024
NB = B // NCORE
T = 512
NT = S // T
DEPTH = 2
NCH = 8
KCONV = 31
KSC = 4
HALO = KCONV - 1
HALO2 = KSC - 1
VW = T + HALO + 2
LW = T + HALO2
EPS = 1e-6
NS = 4
PPL = 16

LP = 408
NG4 = 8
W4 = T + 28
OFF = {'dw_b': 0, 'cln_g': 8, 'cln_b': 16, 'pw2_b': 24, 'sc_b': 32, 'br': 40, 'bi': 48,
       'lam': 56, 'norm_g': 64, 'dww': 72, 'scw': 328, 'mod_b': 360}
OFF_FINAL = DEPTH * LP
OFF_IDENT = OFF_FINAL + 8
OFF_IDB = OFF_IDENT + 128
NP = OFF_IDB + 32

PIECES = ['g0', 'g2', 'g1', 'g3', 'g6', 'g7', 'g8', 'g9', 'g4', 'g5', 'P0', 'P1', 'WO0', 'WO1', 'WO2', 'WO3']


class Reg:
    __slots__ = ('name', 'writer', 'readers')

    def __init__(self, name):
        self.name = name
        self.writer = None
        self.readers = {}


class Prog:
    ENGS = ['pe', 'act', 'dve', 'pool', 'sp']

    def __init__(self, nc, sems, junk):
        self.nc = nc
        self.q = {e: [] for e in self.ENGS}
        self.sem = sems
        self.cnt = {}
        self.ack = {e: {} for e in self.ENGS}
        self.junk = junk
        self.jk = 0
        self.nops = {e: 0 for e in self.ENGS}
        self.cur = {e: e for e in self.ENGS}
        self.eng_of = {e: e for e in self.ENGS}

    def new_epoch(self, E, key, semh):
        self.sem[key] = semh
        self.cur[E] = key
        self.eng_of[key] = E

    def _deps(self, reads, writes):
        deps = {}
        for r in reads:
            if r.writer is not None:
                k, n = r.writer
                if n > deps.get(k, 0):
                    deps[k] = n
        for w in writes:
            if w.writer is not None:
                k, n = w.writer
                if n > deps.get(k, 0):
                    deps[k] = n
            for k, n in w.readers.items():
                if n > deps.get(k, 0):
                    deps[k] = n
        return deps

    def _need(self, E, deps):
        for X, n in deps.items():
            if self.eng_of.get(X) == E and (E == 'pe' or n > self.cnt.get(X, 0)):
                continue
            if n > self.ack[E].get(X, 0):
                self.ack[E][X] = n
                sem = self.sem[X]
                self.q[E].append(lambda eng, sem=sem, v=n: eng.wait_ge(sem, v))

    def _mark(self, tag, reads, writes):
        k, n = tag
        for w in writes:
            w.writer = tag
            w.readers = {}
        for r in reads:
            if n > r.readers.get(k, 0):
                r.readers[k] = n

    def dma(self, E, dsem, out, in_, reads=(), writes=()):
        self._need(E, self._deps(reads, writes))
        self.cnt[dsem] = self.cnt.get(dsem, 0) + 16
        tag = (dsem, self.cnt[dsem])
        sem = self.sem[dsem]
        self.q[E].append(lambda eng, sem=sem, out=out, in_=in_: eng.dma_start(out=out, in_=in_).then_inc(sem, 16))
        self._mark(tag, reads, writes)
        self.nops[E] += 1

    def op(self, E, fn, reads=(), writes=(), inc=True):
        self._need(E, self._deps(reads, writes))
        key = self.cur[E]
        tag = (key, self.cnt.get(key, 0) + 1)
        if inc:
            self.cnt[key] = tag[1]
            sem = self.sem[key]
            self.q[E].append(lambda eng, fn=fn, sem=sem: fn(eng).then_inc(sem, 1))
        else:
            self.q[E].append(lambda eng, fn=fn: fn(eng))
        self._mark(tag, reads, writes)
        self.nops[E] += 1

    def barrier(self):
        allk = {k: v for k, v in self.cnt.items()}
        for E in self.ENGS:
            self._need(E, allk)

    def run(self):
        nc = self.nc
        with nc.Block() as block:
            @block.tensor
            def _(e):
                for f in self.q['pe']:
                    f(e)

            @block.scalar
            def _(e):
                for f in self.q['act']:
                    f(e)

            @block.vector
            def _(e):
                for f in self.q['dve']:
                    f(e)

            @block.gpsimd
            def _(e):
                for f in self.q['pool']:
                    f(e)

            @block.sync
            def _(e):
                for f in self.q['sp']:
                    f(e)


def ap3(ap2, a, b, sa=None, sb=1):
    if sa is None:
        sa = b
    return bass.AP(ap2.tensor, ap2.offset, [list(ap2.ap[0]), [sa, a], [sb, b]])


class Rot:
    def __init__(self, tiles, name):
        self.tiles = tiles
        self.regs = [Reg(f"{name}{i}") for i in range(len(tiles))]
        self.i = 0

    def next(self):
        i = self.i % len(self.tiles)
        self.i += 1
        return self.tiles[i], self.regs[i]


class _Stop(Exception):
    pass


def build_nc(NB=NB, NT=NT, DEPTH=DEPTH, stop=None, dump=None):
    S = NT * T
    nc = bass.Bass("TRN2", target_bir_lowering=False)
    xT = nc.dram_tensor("xT", [NB, D, S], F32, kind="ExternalInput").ap()
    cT = nc.dram_tensor("cT", [128, 2 * NCH], F32, kind="ExternalInput").ap()
    prm_d = nc.dram_tensor("prm", [128, NP], F32, kind="ExternalInput").ap()
    w_in_d = nc.dram_tensor("w_in", [DEPTH, D, 5 * D], F32, kind="ExternalInput").ap()
    pw2_d = nc.dram_tensor("pw2_w", [DEPTH, D, D], F32, kind="ExternalInput").ap()
    w_out_d = nc.dram_tensor("w_out", [DEPTH, 2 * D, D], F32, kind="ExternalInput").ap()
    wr_d = nc.dram_tensor("wr", [DEPTH, 8, 128, 128], F32, kind="ExternalInput").ap()
    wi_d = nc.dram_tensor("wi", [DEPTH, 8, 128, 128], F32, kind="ExternalInput").ap()
    mod_w_d = nc.dram_tensor("mod_w", [DEPTH, D, 3 * D], F32, kind="ExternalInput").ap()
    yT = nc.dram_tensor("yT", [NB, D, S], F32, kind="ExternalOutput").ap()
    wsc = nc.dram_tensor("wsc", [DEPTH * PPL, 128, 4096], BF16, kind="Internal").ap()
    vscr = nc.dram_tensor("vscr", [3, 128, VW], BF16, kind="Internal").ap()

    from contextlib import ExitStack
    with ExitStack() as st:
        def sb(name, shape, dt):
            return st.enter_context(nc.sbuf_tensor("sb_" + name, shape, dt))

        def sem(name):
            return st.enter_context(nc.semaphore(name))

        prm = sb("prm", [128, NP], F32)
        xres = sb("xres", [128, NCH * T], F32)
        import os
        xres2 = sb("xres2", [128, NCH * T], F32) if os.environ.get('DBG_XDST2') else None
        vc = sb("vc", [128, NCH * T], F32)
        stat = [sb(f"stat{i}", [128, T], F32) for i in range(6)]
        tf_t = [sb(f"tf{i}", [128, T], F32) for i in range(14)]
        modsb = sb("modsb", [128, DEPTH * 48], F32)
        Amod = sb("Amod", [128, DEPTH * NB * NCH], F32)
        Gmod = sb("Gmod", [128, DEPTH * NB * NCH], F32)
        Kc = sb("Kc", [128, DEPTH * NCH], F32)
        K2 = sb("K2", [128, DEPTH * NCH], F32)
        ktmp = sb("ktmp", [128, DEPTH * NCH], F32)
        Kh = sb("Kh", [128, DEPTH * NCH], F32)
        hb = sb("hb", [128, DEPTH * 2 * NCH], F32)
        cact = sb("cact", [128, 2 * NCH], F32)
        hstate = sb("hstate", [128, DEPTH * NCH], F32)
        cst = sb("cst", [128, 4], F32)

        h = sb("h", [128, NCH * T], BF16)
        v = sb("v", [128, NCH * VW], BF16)
        vtail = sb("vtail", [128, DEPTH * NCH * HALO], BF16)
        vn = sb("vn", [128, NCH * T], BF16)
        yconv = sb("yconv", [128, NCH * T], BF16)
        ylru = sb("ylru", [128, NCH * T], BF16)
        lxb = sb("lxb", [128, NCH * LW], BF16)
        ltail = sb("ltail", [128, DEPTH * NCH * HALO2], BF16)
        xsb = sb("xsb", [128, NCH * T], BF16)
        scs_all = sb("scs_all", [128, NCH * T], BF16)
        ident = sb("ident", [128, 128], BF16)
        onesm = sb("onesm", [128, 128], BF16)
        dg_t = [sb(f"dg{i}", [128, NG4 * 128], BF16) for i in range(2)]
        identb = sb("identb", [128, 32], BF16)
        v4_t = [sb(f"v4_{i}", [128, 4 * W4], BF16) for i in range(3)]
        sdg_t = [sb(f"sdg{i}", [128, KSC * 128], BF16) for i in range(2)]
        rotb_t = [sb(f"rotb{i}", [128, T], BF16) for i in range(6)]
        gates = sb("gates", [128, DEPTH * 2048], BF16)
        ring = [sb(f"ring{i}", [128, 4096], BF16) for i in range(NS)]

        ps = [st.enter_context(nc.psum_tensor(f"ps{i}", [128, T], F32)) for i in range(8)]

        sems = {e: sem(f"s_{e}") for e in ['pe', 'act', 'dve', 'pool']}
        junk = [sem(f"junk{i}") for i in range(4)]
        P = Prog(nc, sems, junk)

        def dsem(name):
            P.sem[name] = sem("d_" + name)
            return name

        ep_sems = [{e: sem(f"s_{e}_ep{i}") for e in ['pe', 'act', 'dve', 'pool']} for i in range(NB * NT)]
        d_misc = dsem("misc")
        d_misc2 = dsem("misc2")
        d_stg = [dsem(f"stg{i}") for i in range(2)]
        d_sto = [dsem(f"sto{i}") for i in range(NS)]
        d_ring = [dsem(f"ring{i}") for i in range(NS)]
        d_v4 = [dsem(f"v4_{i}") for i in range(3)]
        d_v4w = [dsem(f"v4w_{i}") for i in range(3)]
        r_vscr = [Reg(f"vscr{i}") for i in range(3)]
        r_identb = Reg("identb")
        d_x = dsem("x")
        d_y = dsem("y")

        r_prm = Reg("prm")
        r_x = [Reg(f"x{c}") for c in range(NCH)]
        r_vc = [Reg(f"vc{c}") for c in range(NCH)]
        r_stat = [Reg(f"stat{i}") for i in range(6)]
        r_mod = Reg("mod")
        r_K = Reg("K")
        r_cact = Reg("cact")
        r_hst = [[Reg(f"hst{l}_{c}") for c in range(NCH)] for l in range(DEPTH)]
        r_cst = Reg("cst")
        r_h = [Reg(f"h{c}") for c in range(NCH)]
        r_v = [Reg(f"v{c}") for c in range(NCH)]
        r_vtail = [Reg(f"vtail{l}") for l in range(DEPTH)]
        r_vn = [Reg(f"vn{c}") for c in range(NCH)]
        r_yc = [Reg(f"yc{c}") for c in range(NCH)]
        r_yl = [Reg(f"yl{c}") for c in range(NCH)]
        r_lxb = [Reg(f"lxb{c}") for c in range(NCH)]
        r_ltail = [Reg(f"ltail{l}") for l in range(DEPTH)]
        r_xsb = [Reg(f"xsb{c}") for c in range(NCH)]
        r_scs = [Reg(f"scs{c}") for c in range(NCH)]
        r_ident = Reg("ident")
        r_ones = Reg("ones")
        r_gates = Reg("gates")
        r_ring = [Reg(f"ring{i}") for i in range(NS)]
        r_wsc = [Reg(f"wsc{i}") for i in range(DEPTH * PPL)]
        r_ps = [Reg(f"ps{i}") for i in range(8)]

        tf = Rot([t[:, :] for t in tf_t], "tf")
        rotb = Rot([t[:, :] for t in rotb_t], "rotb")
        dgr = Rot([t[:, :] for t in dg_t], "dg")
        sdgr = Rot([t[:, :] for t in sdg_t], "sdg")
        psA = Rot([ps[i][:, :] for i in range(4)], "psA")
        psA.regs = r_ps[0:4]
        psC = Rot([ps[i][:, :] for i in (4, 5)], "psC")
        psC.regs = r_ps[4:6]
        psS = [ps[6][:, :], ps[7][:, :]]
        r_psS = r_ps[6:8]

        def ch(t, c, w=T):
            return t[:, c * w:(c + 1) * w]

        def pcol(l, name, c):
            o = l * LP + OFF[name] + c
            return prm[:, o:o + 1]

        _stg = ['p0', 'p1', 'p2', 'p3']
        lvl = _stg.index(stop) if stop in _stg else 99
        P.dma('sp', d_misc, prm[:, :], prm_d, writes=[r_prm])
        P.dma('sp', d_misc2, cact[:, :], cT, writes=[r_cact])
        P.op('pool', lambda e: e.memset(cst[:, 0:1], EPS), writes=[r_cst])
        P.op('pool', lambda e: e.memset(cst[:, 1:2], 1.0), writes=[r_cst])
        P.op('pool', lambda e: e.memset(cst[:, 2:3], -0.6931471805599453), writes=[r_cst])
        P.op('pool', lambda e: e.memset(onesm[:, :], 1.0 / D), writes=[r_ones])
        P.op('pool', lambda e: e.memset(hstate[:, :], 0.0), writes=[x for l in r_hst for x in l])
        P.op('dve', lambda e: e.tensor_copy(out=ident[:, :], in_=prm[:, OFF_IDENT:OFF_IDENT + 128]),
             reads=[r_prm], writes=[r_ident])
        P.op('dve', lambda e: e.tensor_copy(out=identb[:, :], in_=prm[:, OFF_IDB:OFF_IDB + 32]),
             reads=[r_prm], writes=[r_identb])
        P.op('pool', lambda e: e.memset(v[:, :], 0.0), writes=r_v)
        v4r = Rot([t for t in v4_t], "v4")
        for _i in range(3):
            P.op('pool', lambda e, _i=_i: e.memset(v4_t[_i][:, :], 0.0), writes=[v4r.regs[_i]])
        P.op('act', lambda e: e.activation(out=cact[:, :], in_=cact[:, :], func=AF.Silu), reads=[r_cact], writes=[r_cact])
        for l in range(DEPTH):
            lam_ap = prm[:, l * LP + OFF['lam']: l * LP + OFF['lam'] + NCH]
            kt = ktmp[:, l * NCH:(l + 1) * NCH]
            P.op('act', lambda e, lam_ap=lam_ap, kt=kt: e.activation(out=kt, in_=lam_ap, func=AF.Exp, scale=-1.0),
                 reads=[r_prm], writes=[r_K])
            P.op('act', lambda e, kt=kt: e.activation(out=kt, in_=kt, func=AF.Ln, bias=cst[:, 1:2]),
                 reads=[r_K, r_cst], writes=[r_K])
            P.op('dve', lambda e, kt=kt, l=l: e.tensor_scalar(out=Kc[:, l * NCH:(l + 1) * NCH], in0=kt, scalar1=-8.0,
                                                           scalar2=None, op0=ALU.mult), reads=[r_K], writes=[r_K])
            P.op('dve', lambda e, kt=kt, l=l: e.tensor_scalar(out=K2[:, l * NCH:(l + 1) * NCH], in0=kt, scalar1=-16.0,
                                                           scalar2=None, op0=ALU.mult), reads=[r_K], writes=[r_K])
            P.op('dve', lambda e, kt=kt, l=l: e.tensor_scalar(out=Kh[:, l * NCH:(l + 1) * NCH], in0=kt, scalar1=-4.0,
                                                           scalar2=None, op0=ALU.mult), reads=[r_K], writes=[r_K])
            P.op('dve', lambda e, l=l: e.tensor_scalar(out=hb[:, l * 2 * NCH:(l + 1) * 2 * NCH],
                                                     in0=prm[:, l * LP + OFF['br']: l * LP + OFF['br'] + 2 * NCH], scalar1=0.5,
                                                     scalar2=None, op0=ALU.mult), reads=[r_prm], writes=[r_K])

        stg = [xres[:, :], vc[:, :]]
        r_stg = [Reg("stg0"), Reg("stg1")]
        stg_i = [0]

        def next_stage():
            i = stg_i[0] % 2
            stg_i[0] += 1
            return i

        for l in (range(DEPTH) if lvl >= 1 else []):
            mps = ps[0][:, 0:48]
            for g in range(6):
                si = next_stage()
                src = mod_w_d[l].rearrange("(kc p) n -> p kc n", p=128)[:, :, g * 512:(g + 1) * 512]
                P.dma('sp', d_stg[si], ap3(stg[si], NCH, 512), src, writes=[r_stg[si]])
                import os
                for m in (range(4) if not os.environ.get('DBG_NOMODMM') else []):
                    oc = g * 4 + m
                    for kc in range(NCH):
                        lhsT = stg[si][:, kc * 512 + m * 128: kc * 512 + (m + 1) * 128]
                        rhs = cact[:, kc * 2:(kc + 1) * 2]
                        P.op('pe', lambda e, lhsT=lhsT, rhs=rhs, kc=kc, oc=oc: e.matmul(
                            ps[0][:, oc * 2:(oc + 1) * 2], lhsT, rhs, start=(kc == 0), stop=(kc == NCH - 1)),
                            reads=[r_stg[si], r_cact], writes=[r_ps[0]], inc=True)
            mb = prm[:, l * LP + OFF['mod_b']: l * LP + OFF['mod_b'] + 48]
            P.op('dve', lambda e, l=l, mb=mb, mps=mps: e.tensor_tensor(out=modsb[:, l * 48:(l + 1) * 48], in0=mps, in1=mb,
                                                                   op=ALU.add), reads=[r_ps[0], r_prm], writes=[r_mod])
            for b in range(NB):
                base = modsb[:, l * 48 + 16 + b: l * 48 + 16 + b + 1]
                scale_v = bass.AP(base.tensor, base.offset, [list(base.ap[0]), [2, NCH]])
                base2 = modsb[:, l * 48 + 32 + b: l * 48 + 32 + b + 1]
                gate_v = bass.AP(base2.tensor, base2.offset, [list(base2.ap[0]), [2, NCH]])
                o = (l * NB + b) * NCH
                ng = prm[:, l * LP + OFF['norm_g']: l * LP + OFF['norm_g'] + NCH]
                P.op('dve', lambda e, o=o, scale_v=scale_v, ng=ng: e.scalar_tensor_tensor(
                    out=Amod[:, o:o + NCH], in0=scale_v, scalar=1.0, in1=ng, op0=ALU.add, op1=ALU.mult),
                    reads=[r_mod, r_prm], writes=[r_mod])
                P.op('dve', lambda e, o=o, gate_v=gate_v: e.tensor_scalar(
                    out=Gmod[:, o:o + NCH], in0=gate_v, scalar1=1.0, scalar2=None, op0=ALU.add),
                    reads=[r_mod], writes=[r_mod])

        def shift_col(l, b, c):
            o = l * 48 + c * 2 + b
            return modsb[:, o:o + 1]

        for l in (range(DEPTH) if lvl >= 2 else []):
            si = next_stage()
            P.dma('sp', d_stg[si], ap3(stg[si][:, 0:1024], 8, 128), wr_d[l].rearrange("h d e -> d h e"), writes=[r_stg[si]])
            P.dma('sp', d_stg[si], ap3(stg[si][:, 1024:2048], 8, 128), wi_d[l].rearrange("h d e -> d h e"), writes=[r_stg[si]])
            P.op('dve', lambda e, l=l, si=si: e.tensor_copy(out=gates[:, l * 2048:(l + 1) * 2048], in_=stg[si][:, 0:2048]),
                 reads=[r_stg[si]], writes=[r_gates])

        def piece_src(l, name):
            if name[0] == 'g':
                g = int(name[1:])
                return w_in_d[l].rearrange("(kc p) n -> p kc n", p=128)[:, :, g * 512:(g + 1) * 512], (NCH, 512)
            if name[0] == 'P':
                q = int(name[1:])
                return pw2_d[l].rearrange("(kc p) n -> p kc n", p=128)[:, :, q * 512:(q + 1) * 512], (NCH, 512)
            q = int(name[2:])
            return w_out_d[l].rearrange("(kc p) n -> p kc n", p=128)[:, :, q * 256:(q + 1) * 256], (2 * NCH, 256)

        cast_engs = ['act', 'dve']
        pi = 0
        for l in (range(DEPTH) if lvl >= 3 else []):
            for k, name in enumerate(PIECES):
                pid = l * PPL + k
                si = next_stage()
                src, (a, bb) = piece_src(l, name)
                P.dma('sp', d_stg[si], ap3(stg[si], a, bb), src, writes=[r_stg[si]])
                rs = pi % NS
                E = cast_engs[pi % 2]
                if E == 'act':
                    P.op('act', lambda e, rs=rs, si=si: e.activation(out=ring[rs][:, :], in_=stg[si], func=AF.Copy),
                         reads=[r_stg[si]], writes=[r_ring[rs]])
                else:
                    P.op(E, lambda e, rs=rs, si=si: e.tensor_copy(out=ring[rs][:, :], in_=stg[si]),
                         reads=[r_stg[si]], writes=[r_ring[rs]])
                P.dma('sp', d_sto[rs], wsc[pid], ring[rs][:, :], reads=[r_ring[rs]], writes=[r_wsc[pid]])
                pi += 1

        P.barrier()

        stream_len = NB * NT * DEPTH * PPL
        state = {'issued': 0, 'next': 0, 'retired': 0}

        def ring_issue(m):
            tl = m // PPL
            l = tl % DEPTH
            pid = l * PPL + (m % PPL)
            s = m % NS
            P.dma('sp', d_ring[s], ring[s][:, :], wsc[pid], reads=[r_wsc[pid]], writes=[r_ring[s]])

        def ring_fill():
            while state['issued'] < stream_len and state['issued'] < state['retired'] + NS:
                ring_issue(state['issued'])
                state['issued'] += 1

        def ring_acquire():
            m = state['next']
            state['next'] += 1
            assert m < state['issued'], "ring piece not issued before use"
            return m % NS

        def ring_retire(n=1):
            state['retired'] += n
            ring_fill()

        ring_fill()

        def mm(out_ap, out_reg, lhsT, lregs, rhs, rregs, start, stop, tile_position=None):
            P.op('pe', lambda e: e.matmul(out_ap, lhsT, rhs, start=start, stop=stop, tile_position=tile_position),
                 reads=list(lregs) + list(rregs), writes=[out_reg], inc=True)

        def rms_stats(src_t, src_regs, rstd_tile, r_rstd):
            for c in range(NCH):
                sq, r_sq = rotb.next()
                P.op('act', lambda e, c=c, sq=sq: e.activation(out=sq, in_=ch(src_t, c), func=AF.Square),
                     reads=[src_regs[c]], writes=[r_sq])
                mm(psS[0], r_psS[0], onesm[:, :], [r_ones], sq, [r_sq], c == 0, c == NCH - 1)
            sd, r_sd = tf.next()
            P.op('act', lambda e: e.activation(out=sd, in_=psS[0], func=AF.Ln, bias=cst[:, 0:1]),
                 reads=[r_psS[0], r_cst], writes=[r_sd])
            P.op('act', lambda e: e.activation(out=rstd_tile, in_=sd, func=AF.Exp, scale=-0.5), reads=[r_sd], writes=[r_rstd])

        tl_total = NB * NT * DEPTH
        dg_fifo, sdg_fifo = [], []
        gen_state = {'dg': 0, 'sdg': 0}

        def gen_next_dg():
            n = gen_state['dg']
            if n >= tl_total * NCH:
                return
            gen_state['dg'] = n + 1
            l_ = (n // NCH) % DEPTH
            c_ = n % NCH
            dg, r_dg = dgr.next()
            dwo = l_ * LP + OFF['dww'] + c_ * NG4 * 4
            dwb = prm[:, dwo:dwo + 1]
            ib = identb[:, :]
            in0 = bass.AP(ib.tensor, ib.offset, [list(ib.ap[0]), [0, NG4 * 4], [1, 32]])
            in1 = bass.AP(dwb.tensor, dwb.offset, [list(dwb.ap[0]), [1, NG4 * 4], [0, 32]])
            P.op('pool', lambda e, dg=dg, in0=in0, in1=in1: e.tensor_tensor(
                out=ap3(dg, NG4 * 4, 32), in0=in0, in1=in1, op=ALU.mult),
                reads=[r_identb, r_prm], writes=[r_dg])
            dg_fifo.append((dg, r_dg))

        v4_fifo = []
        import os as _os
        V4Q = _os.environ.get('DBG_V4Q', 'sp')

        def build_v4(c_):
            sl = v4r.i % 3
            vt, r_vt = v4r.next()
            P.dma(V4Q, d_v4w[sl], vscr[sl], v[:, c_ * VW:(c_ + 1) * VW], reads=[r_v[c_]], writes=[r_vscr[sl]])
            sa = vscr[sl]
            for ii in range(4):
                src = bass.AP(sa.tensor, sa.offset + ii, [[VW, 32], [32 * VW, 4], [1, W4]])
                dd = vt[32 * ii:32 * (ii + 1), :]
                dst = bass.AP(dd.tensor, dd.offset, [list(dd.ap[0]), [W4, 4], [1, W4]])
                P.dma(V4Q, d_v4[sl], dst, src, reads=[r_vscr[sl]], writes=[r_vt])
            v4_fifo.append((vt, r_vt))

        def gen_next_sdg():
            n = gen_state['sdg']
            if n >= tl_total * NCH:
                return
            gen_state['sdg'] = n + 1
            l_ = (n // NCH) % DEPTH
            c_ = n % NCH
            sdg, r_sdg = sdgr.next()
            swo = l_ * LP + OFF['scw'] + c_ * KSC
            swb = prm[:, swo:swo + 1]
            in0 = bass.AP(ident[:, :].tensor, ident[:, :].offset, [list(ident[:, :].ap[0]), [0, KSC], [1, 128]])
            in1 = bass.AP(swb.tensor, swb.offset, [list(swb.ap[0]), [1, KSC], [0, 128]])
            P.op('pool', lambda e, sdg=sdg, in0=in0, in1=in1: e.tensor_tensor(
                out=ap3(sdg, KSC, 128), in0=in0, in1=in1, op=ALU.mult), reads=[r_ident, r_prm], writes=[r_sdg])
            sdg_fifo.append((sdg, r_sdg))

        for _ in range(2):
            gen_next_dg()
            gen_next_sdg()

        pending_out = []

        def flush_out():
            while pending_out:
                dst_, src_, regs_ = pending_out.pop(0)
                P.dma('sp', d_y, dst_, src_, reads=regs_)

        bufs2 = [(xres, r_x), (vc, r_vc)]
        total_tiles = NB * NT

        def load_x(tt_):
            Xb, r_Xb = bufs2[tt_ % 2]
            b_, j_ = tt_ // NT, tt_ % NT
            P.dma('sp', d_x, ap3(Xb[:, :], NCH, T), xT[b_].rearrange("(c p) s -> p c s", p=128)[:, :, j_ * T:(j_ + 1) * T],
                  writes=r_Xb)

        def phase_A(tt_, l_):
            Xb, r_Xb = bufs2[tt_ % 2]
            b_ = tt_ // NT
            rstd_x = stat[0][:, :]
            rms_stats(Xb, r_Xb, rstd_x, r_stat[0])
            for c in range(NCH):
                t1, r_t1 = tf.next()
                acol = Amod[:, (l_ * NB + b_) * NCH + c:(l_ * NB + b_) * NCH + c + 1]
                P.op('dve', lambda e, c=c, t1=t1, acol=acol, Xb=Xb: e.scalar_tensor_tensor(
                    out=t1, in0=ch(Xb, c), scalar=acol, in1=rstd_x, op0=ALU.mult, op1=ALU.mult),
                    reads=[r_Xb[c], r_mod, r_stat[0]], writes=[r_t1])
                scol = shift_col(l_, b_, c)
                P.op('act', lambda e, c=c, t1=t1, scol=scol: e.activation(
                    out=ch(h, c), in_=t1, func=AF.Identity, bias=scol),
                    reads=[r_t1, r_mod], writes=[r_h[c]])

        _occ = {}

        def chk(tag):
            _occ[tag] = _occ.get(tag, 0) + 1
            if stop == tag or (tag == 'prologue' and stop in _stg) or stop == f"{tag}@{_occ[tag] - 1}":
                raise _Stop()

        try:
            chk('prologue')
            for b in range(NB):
                for j in range(NT):
                    first = (j == 0)
                    for _e in ['pe', 'act', 'dve', 'pool']:
                        P.new_epoch(_e, f"{_e}#{b * NT + j}", ep_sems[b * NT + j][_e])
                    cols = slice(j * T, (j + 1) * T)
                    tt = b * NT + j
                    X, r_X = bufs2[tt % 2]
                    V, r_V = bufs2[(tt + 1) % 2]
                    if tt == 0:
                        load_x(0)
                        phase_A(0, 0)
                    chk('X')
                    for l in range(DEPTH):
                        if l > 0:
                            phase_A(tt, l)
                        chk('A')
                        vh = ap3(v[:, :], NCH, HALO, sa=VW)
                        lh = ap3(lxb[:, :], NCH, HALO2, sa=LW)
                        if first:
                            P.op('pool', lambda e, vh=vh: e.memset(vh, 0.0), writes=r_v)
                            P.op('pool', lambda e, lh=lh: e.memset(lh, 0.0), writes=r_lxb)
                        else:
                            vt = ap3(vtail[:, l * NCH * HALO:(l + 1) * NCH * HALO], NCH, HALO)
                            lt = ap3(ltail[:, l * NCH * HALO2:(l + 1) * NCH * HALO2], NCH, HALO2)
                            P.op('pool', lambda e, vh=vh, vt=vt: e.tensor_copy(out=vh, in_=vt), reads=[r_vtail[l]], writes=r_v)
                            P.op('pool', lambda e, lh=lh, lt=lt: e.tensor_copy(out=lh, in_=lt), reads=[r_ltail[l]], writes=r_lxb)

                        for half in range(2):
                            s_cv = ring_acquire()
                            s_cg = ring_acquire()
                            for m in range(4):
                                c = half * 4 + m
                                pa, r_pa = psA.next()
                                pb, r_pb = psA.next()
                                for kc in range(NCH):
                                    mm(pa, r_pa, ring[s_cv][:, kc * 512 + m * 128: kc * 512 + (m + 1) * 128], [r_ring[s_cv]],
                                       ch(h, kc), [r_h[kc]], kc == 0, kc == NCH - 1)
                                for kc in range(NCH):
                                    mm(pb, r_pb, ring[s_cg][:, kc * 512 + m * 128: kc * 512 + (m + 1) * 128], [r_ring[s_cg]],
                                       ch(h, kc), [r_h[kc]], kc == 0, kc == NCH - 1)
                                sg, r_sg = tf.next()
                                P.op('act', lambda e, sg=sg, pb=pb: e.activation(out=sg, in_=pb, func=AF.Sigmoid),
                                     reads=[r_pb], writes=[r_sg])
                                vdst = v[:, c * VW + HALO: c * VW + HALO + T]
                                P.op('dve', lambda e, vdst=vdst, pa=pa, sg=sg: e.tensor_tensor(out=vdst, in0=pa, in1=sg, op=ALU.mult),
                                     reads=[r_pa, r_sg], writes=[r_v[c]])
                                if c < 3:
                                    build_v4(c)
                            ring_retire(2)
                        vt = ap3(vtail[:, l * NCH * HALO:(l + 1) * NCH * HALO], NCH, HALO)
                        vlast = bass.AP(v[:, :].tensor, v[:, T:T + 1].offset, [list(v[:, :].ap[0]), [VW, NCH], [1, HALO]])
                        P.op('pool', lambda e, vt=vt, vlast=vlast: e.tensor_copy(out=vt, in_=vlast), reads=r_v, writes=[r_vtail[l]])

                        chk('B')
                        for half in range(2):
                            s_lx = ring_acquire()
                            for m in range(4):
                                c = half * 4 + m
                                pa, r_pa = psA.next()
                                for kc in range(NCH):
                                    mm(pa, r_pa, ring[s_lx][:, kc * 512 + m * 128: kc * 512 + (m + 1) * 128], [r_ring[s_lx]],
                                       ch(h, kc), [r_h[kc]], kc == 0, kc == NCH - 1)
                                ldst = lxb[:, c * LW + HALO2: c * LW + HALO2 + T]
                                P.op('act', lambda e, ldst=ldst, pa=pa: e.activation(out=ldst, in_=pa, func=AF.Copy),
                                     reads=[r_pa], writes=[r_lxb[c]])
                            ring_retire(1)
                        lt = ap3(ltail[:, l * NCH * HALO2:(l + 1) * NCH * HALO2], NCH, HALO2)
                        llast = bass.AP(lxb[:, :].tensor, lxb[:, T:T + 1].offset, [list(lxb[:, :].ap[0]), [LW, NCH], [1, HALO2]])
                        P.op('pool', lambda e, lt=lt, llast=llast: e.tensor_copy(out=lt, in_=llast), reads=r_lxb, writes=[r_ltail[l]])

                        if l == 0:
                            flush_out()
                        chk('E1')
                        s_ls_state = {'s': None}

                        def lru_chunk(c):
                            s_ls = s_ls_state['s']
                            sdg, r_sdg = sdg_fifo.pop(0)
                            pc, r_pc = psC.next()
                            for k in range(KSC):
                                mm(pc, r_pc, sdg[:, k * 128:(k + 1) * 128], [r_sdg],
                                   lxb[:, c * LW + k: c * LW + k + T], [r_lxb[c]], k == 0, k == KSC - 1)
                            gen_next_sdg()
                            scb = pcol(l, 'sc_b', c)
                            P.op('dve', lambda e, c=c, pc=pc, scb=scb: e.tensor_scalar(out=ch(xsb, c), in0=pc, scalar1=scb, scalar2=None, op0=ALU.add),
                                 reads=[r_pc, r_prm], writes=[r_xsb[c]])
                            if c % 4 == 0:
                                s_ls = ring_acquire()
                            m = c % 4
                            pl_, r_pl = psA.next()
                            for kc in range(NCH):
                                mm(pl_, r_pl, ring[s_ls][:, kc * 512 + m * 128: kc * 512 + (m + 1) * 128], [r_ring[s_ls]],
                                   ch(h, kc), [r_h[kc]], kc == 0, kc == NCH - 1)
                            if c % 4 == 3:
                                ring_retire(1)
                            pr_, r_pr = psA.next()
                            pi_, r_pi = psA.next()
                            go = l * 2048
                            mm(pr_, r_pr, gates[:, go + c * 128: go + (c + 1) * 128], [r_gates], ch(xsb, c), [r_xsb[c]], True, True)
                            mm(pi_, r_pi, gates[:, go + 1024 + c * 128: go + 1024 + (c + 1) * 128], [r_gates], ch(xsb, c), [r_xsb[c]], True, True)
                            sls, r_sls = tf.next()
                            P.op('act', lambda e, sls=sls, pl_=pl_: e.activation(out=sls, in_=pl_, func=AF.Silu),
                                 reads=[r_pl], writes=[r_sls])
                            rr, r_rr = tf.next()
                            ig, r_ig = tf.next()
                            hbr = hb[:, l * 2 * NCH + c: l * 2 * NCH + c + 1]
                            hbi = hb[:, l * 2 * NCH + NCH + c: l * 2 * NCH + NCH + c + 1]
                            P.op('act', lambda e, rr=rr, pr_=pr_, hbr=hbr: e.activation(out=rr, in_=pr_, func=AF.Tanh, scale=0.5, bias=hbr),
                                 reads=[r_pr, r_K], writes=[r_rr])
                            P.op('act', lambda e, ig=ig, pi_=pi_, hbi=hbi: e.activation(out=ig, in_=pi_, func=AF.Tanh, scale=0.5, bias=hbi),
                                 reads=[r_pi, r_K], writes=[r_ig])
                            aa, r_aa = tf.next()
                            a2, r_a2 = tf.next()
                            kcol = Kc[:, l * NCH + c: l * NCH + c + 1]
                            khcol = Kh[:, l * NCH + c: l * NCH + c + 1]
                            P.op('act', lambda e, aa=aa, rr=rr, khcol=khcol: e.activation(out=aa, in_=rr, func=AF.Exp, scale=khcol, bias=khcol),
                                 reads=[r_rr, r_K], writes=[r_aa])
                            P.op('act', lambda e, a2=a2, rr=rr, kcol=kcol: e.activation(out=a2, in_=rr, func=AF.Exp, scale=kcol, bias=kcol),
                                 reads=[r_rr, r_K], writes=[r_a2])
                            P.op('act', lambda e, a2=a2: e.activation(out=a2, in_=a2, func=AF.Ln, scale=-1.0, bias=cst[:, 1:2]),
                                 reads=[r_a2, r_cst], writes=[r_a2])
                            P.op('act', lambda e, a2=a2: e.activation(out=a2, in_=a2, func=AF.Exp, scale=0.5, bias=cst[:, 2:3]),
                                 reads=[r_a2, r_cst], writes=[r_a2])
                            s_ls_state['s'] = s_ls
                            return (c, ig, r_ig, a2, r_a2, aa, r_aa, sls, r_sls)

                        def lru_tail(c, ig, r_ig, a2, r_a2, aa, r_aa, sls, r_sls):
                            P.op('dve', lambda e, ig=ig, c=c: e.scalar_tensor_tensor(out=ig, in0=ig, scalar=1.0, in1=ch(xsb, c),
                                                                                  op0=ALU.add, op1=ALU.mult),
                                 reads=[r_ig, r_xsb[c]], writes=[r_ig])
                            P.op('dve', lambda e, ig=ig, a2=a2: e.tensor_tensor(out=ig, in0=ig, in1=a2, op=ALU.mult),
                                 reads=[r_ig, r_a2], writes=[r_ig])
                            hs, r_hs = tf.next()
                            hcol = hstate[:, l * NCH + c: l * NCH + c + 1]
                            if first:
                                P.op('dve', lambda e, hs=hs, aa=aa, ig=ig: e.tensor_tensor_scan(
                                    out=hs, data0=aa, data1=ig, initial=0.0, op0=ALU.mult, op1=ALU.add),
                                    reads=[r_aa, r_ig], writes=[r_hs])
                            else:
                                P.op('dve', lambda e, hs=hs, aa=aa, ig=ig, hcol=hcol: e.tensor_tensor_scan(
                                    out=hs, data0=aa, data1=ig, initial=hcol, op0=ALU.mult, op1=ALU.add),
                                    reads=[r_aa, r_ig, r_hst[l][c]], writes=[r_hs])
                            P.op('pool', lambda e, hs=hs, hcol=hcol: e.tensor_copy(out=hcol, in_=hs[:, T - 1:T]),
                                 reads=[r_hs], writes=[r_hst[l][c]])
                            P.op('dve', lambda e, c=c, hs=hs, sls=sls: e.tensor_tensor(out=ch(ylru, c), in0=hs, in1=sls, op=ALU.mult),
                                 reads=[r_hs, r_sls], writes=[r_yl[c]])

                        pendq = []
                        lru_pending = None
                        for c in range(NCH + 1):
                            if c < NCH:
                                dg, r_dg = dg_fifo.pop(0)
                                vt, r_vt = v4_fifo.pop(0)
                                pci = 4 + (psC.i % 2)
                                pc, r_pc = psC.next()
                                for g in range(NG4):
                                    for jj in range(4):
                                        mm(ps[pci][32 * jj:32 * (jj + 1), :], r_pc, dg[:, (g * 4 + jj) * 32:(g * 4 + jj + 1) * 32], [r_dg],
                                           vt[:, jj * W4 + 4 * g: jj * W4 + 4 * g + T], [r_vt], g == 0, g == NG4 - 1,
                                           tile_position=(0, 32 * jj))
                                gen_next_dg()
                                if c + 3 < NCH:
                                    build_v4(c + 3)
                                bcol = pcol(l, 'dw_b', c)
                                P.op('dve', lambda e, c=c, pc=pc, bcol=bcol, V=V: e.tensor_scalar(out=ch(V, c), in0=pc, scalar1=bcol, scalar2=None, op0=ALU.add),
                                     reads=[r_pc, r_prm], writes=[r_V[c]])
                                sq, r_sq = rotb.next()
                                P.op('dve', lambda e, c=c, sq=sq, V=V: e.tensor_tensor(out=sq, in0=ch(V, c), in1=ch(V, c), op=ALU.mult),
                                     reads=[r_V[c]], writes=[r_sq])
                                vb, r_vb = rotb.next()
                                P.op('dve', lambda e, c=c, vb=vb, V=V: e.tensor_copy(out=vb, in_=ch(V, c)), reads=[r_V[c]], writes=[r_vb])
                                lru_args = lru_chunk(c)
                                if lru_pending is not None:
                                    lru_tail(*lru_pending)
                                lru_pending = lru_args
                            if c < NCH:
                                pendq.append((c, vb, r_vb, sq, r_sq))
                            while pendq and (len(pendq) > 2 or c >= NCH):
                                pcn, pvb, pr_vb, psq, pr_sq = pendq.pop(0)
                                mm(psS[0], r_psS[0], onesm[:, :], [r_ones], pvb, [pr_vb], pcn == 0, pcn == NCH - 1)
                                mm(psS[1], r_psS[1], onesm[:, :], [r_ones], psq, [pr_sq], pcn == 0, pcn == NCH - 1)
                        lru_tail(*lru_pending)
                        mean_sb, var_sb, sd_sb, rstd_sb, mr_sb = [stat[i][:, :] for i in range(1, 6)]
                        P.op('act', lambda e: e.activation(out=mean_sb, in_=psS[0], func=AF.Copy), reads=[r_psS[0]], writes=[r_stat[1]])
                        P.op('dve', lambda e: e.tensor_tensor(out=var_sb, in0=mean_sb, in1=mean_sb, op=ALU.mult),
                             reads=[r_stat[1]], writes=[r_stat[2]])
                        P.op('dve', lambda e: e.tensor_tensor(out=var_sb, in0=psS[1], in1=var_sb, op=ALU.subtract),
                             reads=[r_psS[1], r_stat[2]], writes=[r_stat[2]])
                        P.op('dve', lambda e: e.tensor_scalar(out=var_sb, in0=var_sb, scalar1=0.0, scalar2=None, op0=ALU.max),
                             reads=[r_stat[2]], writes=[r_stat[2]])
                        P.op('act', lambda e: e.activation(out=sd_sb, in_=var_sb, func=AF.Ln, bias=cst[:, 0:1]),
                             reads=[r_stat[2], r_cst], writes=[r_stat[3]])
                        P.op('act', lambda e: e.activation(out=rstd_sb, in_=sd_sb, func=AF.Exp, scale=-0.5), reads=[r_stat[3]], writes=[r_stat[4]])
                        P.op('dve', lambda e: e.scalar_tensor_tensor(out=mr_sb, in0=mean_sb, scalar=-1.0, in1=rstd_sb,
                                                                    op0=ALU.mult, op1=ALU.mult),
                             reads=[r_stat[1], r_stat[4]], writes=[r_stat[5]])

                        chk('C')
                        s_cs = None
                        for c in range(NCH):
                            if c % 4 == 0:
                                s_cs = ring_acquire()
                            m = c % 4
                            pb, r_pb = psA.next()
                            for kc in range(NCH):
                                mm(pb, r_pb, ring[s_cs][:, kc * 512 + m * 128: kc * 512 + (m + 1) * 128], [r_ring[s_cs]],
                                   ch(h, kc), [r_h[kc]], kc == 0, kc == NCH - 1)
                            if c % 4 == 3:
                                ring_retire(1)
                            P.op('act', lambda e, c=c, pb=pb: e.activation(out=ch(scs_all, c), in_=pb, func=AF.Silu),
                                 reads=[r_pb], writes=[r_scs[c]])
                            t1, r_t1 = tf.next()
                            P.op('dve', lambda e, c=c, t1=t1, V=V: e.tensor_tensor(out=t1, in0=ch(V, c), in1=rstd_sb, op=ALU.mult),
                                 reads=[r_V[c], r_stat[4]], writes=[r_t1])
                            P.op('dve', lambda e, t1=t1: e.tensor_tensor(out=t1, in0=t1, in1=mr_sb, op=ALU.add),
                                 reads=[r_t1, r_stat[5]], writes=[r_t1])
                            P.op('act', lambda e, c=c, t1=t1, l=l: e.activation(out=ch(vn, c), in_=t1, func=AF.Silu,
                                                                              scale=pcol(l, 'cln_g', c), bias=pcol(l, 'cln_b', c)),
                                 reads=[r_t1, r_prm], writes=[r_vn[c]])

                        if l == DEPTH - 1 and tt + 1 < total_tiles:
                            load_x(tt + 1)
                        chk('E2')
                        for half in range(2):
                            s_pw = ring_acquire()
                            for m in range(4):
                                c = half * 4 + m
                                pa, r_pa = psA.next()
                                for kc in range(NCH):
                                    mm(pa, r_pa, ring[s_pw][:, kc * 512 + m * 128: kc * 512 + (m + 1) * 128], [r_ring[s_pw]],
                                       ch(vn, kc), [r_vn[kc]], kc == 0, kc == NCH - 1)
                                P.op('dve', lambda e, c=c, pa=pa, l=l: e.scalar_tensor_tensor(
                                    out=ch(yconv, c), in0=pa, scalar=pcol(l, 'pw2_b', c), in1=ch(scs_all, c), op0=ALU.add, op1=ALU.mult),
                                    reads=[r_pa, r_scs[c], r_prm], writes=[r_yc[c]])
                            ring_retire(1)

                        chk('D')
                        if l == DEPTH - 1 and tt + 1 < total_tiles:
                            phase_A(tt + 1, 0)
                        for q in range(4):
                            s_wo = ring_acquire()
                            for m in range(2):
                                c = q * 2 + m
                                pa, r_pa = psA.next()
                                for kc in range(2 * NCH):
                                    src_t, src_r = (yconv, r_yc) if kc < NCH else (ylru, r_yl)
                                    mm(pa, r_pa, ring[s_wo][:, kc * 256 + m * 128: kc * 256 + (m + 1) * 128], [r_ring[s_wo]],
                                       ch(src_t, kc % NCH), [src_r[kc % NCH]], kc == 0, kc == 2 * NCH - 1)
                                gcol = Gmod[:, (l * NB + b) * NCH + c:(l * NB + b) * NCH + c + 1]
                                P.op('dve', lambda e, c=c, pa=pa, gcol=gcol, X=X: e.scalar_tensor_tensor(
                                    out=ch(X, c), in0=pa, scalar=gcol, in1=ch(X, c), op0=ALU.mult, op1=ALU.add),
                                    reads=[r_pa, r_mod, r_X[c]], writes=[r_X[c]])
                            ring_retire(1)

                    chk('F')
                    rstd_x = stat[0][:, :]
                    rms_stats(X, r_X, rstd_x, r_stat[0])
                    for c in range(NCH):
                        fg = prm[:, OFF_FINAL + c: OFF_FINAL + c + 1]
                        P.op('dve', lambda e, c=c, fg=fg, X=X: e.scalar_tensor_tensor(
                            out=ch(X, c), in0=ch(X, c), scalar=fg, in1=rstd_x, op0=ALU.mult, op1=ALU.mult),
                            reads=[r_X[c], r_prm, r_stat[0]], writes=[r_X[c]])
                    pending_out.append((yT[b].rearrange("(c p) s -> p c s", p=128)[:, :, cols], ap3(X[:, :], NCH, T), r_X))
                    chk('N')
            flush_out()

        except _Stop:
            P.barrier()
            if dump == 'nodma':
                raise_flag = True
            else:
                raise_flag = False
            if dump is not None:
                bufs = {'h': h, 'v': v, 'lxb': lxb, 'vn': vn, 'xsb': xsb, 'ylru': ylru, 'yconv': yconv, 'xres': xres,
                        'stat1': stat[1], 'stat4': stat[4], 'stat0': stat[0], 'modsb': modsb, 'Amod': Amod, 'Gmod': Gmod, 'Kc': Kc, 'gates': gates}
                if dump not in ('vc', 'nodma'):
                    src = bufs[dump]
                    w = min(src[:, :].shape[1], NCH * T)
                    P.op('dve', lambda e: e.tensor_copy(out=vc[:, 0:w], in_=src[:, 0:w]), writes=r_vc)
            if not raise_flag:
                P.dma('sp', d_y, yT[0].rearrange("(c p) s -> p c s", p=128)[:, :, 0:T], ap3(vc[:, :], NCH, T), reads=r_vc)
        P.q['sp'].append(lambda e, s=P.sem[d_y], vv=P.cnt[d_y]: e.wait_ge(s, vv))
        P.run()
    return nc


_NC_CACHE = {}


def _vec(vv):
    return np.ascontiguousarray(np.asarray(vv, np.float32).reshape(NCH, 128).T)


def _prep_prm(inp):
    prm = np.zeros((128, NP), np.float32)
    for l in range(DEPTH):
        base = l * LP
        for name in ['dw_b', 'cln_g', 'cln_b', 'pw2_b', 'sc_b', 'br', 'bi', 'lam', 'norm_g']:
            prm[:, base + OFF[name]: base + OFF[name] + NCH] = _vec(inp[name][l])
        wpad = np.zeros((4 * NG4, D), np.float32)
        wpad[:KCONV] = np.asarray(inp['dw_w'][l], np.float32)
        w5 = wpad.reshape(NG4, 4, NCH, 4, 32)
        tab = w5.transpose(1, 4, 2, 0, 3).reshape(128, NCH * NG4 * 4)
        prm[:, base + OFF['dww']: base + OFF['dww'] + NCH * NG4 * 4] = tab
        scw = np.asarray(inp['sc_w'][l], np.float32).reshape(KSC, NCH, 128).transpose(2, 1, 0)
        prm[:, base + OFF['scw']: base + OFF['scw'] + NCH * KSC] = scw.reshape(128, NCH * KSC)
        mb = np.asarray(inp['mod_b'][l], np.float32).reshape(24, 128).T
        prm[:, base + OFF['mod_b']: base + OFF['mod_b'] + 48] = np.repeat(mb, 2, axis=1)
    prm[:, OFF_FINAL:OFF_FINAL + NCH] = _vec(inp['final_g'])
    prm[:, OFF_IDENT:OFF_IDENT + 128] = np.eye(128, dtype=np.float32)
    prm[:, OFF_IDB:OFF_IDB + 32] = np.tile(np.eye(32, dtype=np.float32), (4, 1))
    return prm


def kernel(**inputs):
    inp = {k: np.asarray(v) for k, v in inputs.items()}
    if 'nc' not in _NC_CACHE:
        _NC_CACHE['nc'] = build_nc()
    nc = _NC_CACHE['nc']
    x = inp['x'].astype(np.float32, copy=False)
    c = inp['c'].astype(np.float32, copy=False)
    prm = _prep_prm(inp)
    shared = {
        'prm': prm,
        'w_in': np.ascontiguousarray(inp['w_in'], np.float32),
        'pw2_w': np.ascontiguousarray(inp['pw2_w'], np.float32),
        'w_out': np.ascontiguousarray(inp['w_out'], np.float32),
        'wr': np.ascontiguousarray(inp['wr'], np.float32),
        'wi': np.ascontiguousarray(inp['wi'], np.float32),
        'mod_w': np.ascontiguousarray(inp['mod_w'], np.float32),
    }
    in_maps = []
    for i in range(NCORE):
        xs = x[i * NB:(i + 1) * NB]
        xT = np.ascontiguousarray(xs.transpose(0, 2, 1))
        cs = c[i * NB:(i + 1) * NB]
        cT = np.ascontiguousarray(cs.reshape(NB, NCH, 128).transpose(2, 1, 0).reshape(128, NCH * NB))
        m = dict(shared)
        m['xT'] = xT
        m['cT'] = cT
        in_maps.append(m)
    res = run_bass_kernel_spmd(nc, in_maps, core_ids=list(range(NCORE)))
    out = np.empty((B, S, D), np.float32)
    for i in range(NCORE):
        yT = np.asarray(res.results[i]['yT'])
        out[i * NB:(i + 1) * NB] = yT.transpose(0, 2, 1)
    return out
```
